# Optimizing a Trainium2 kernel written in Bass

```python
import jax, jax.numpy as jnp
from jax import lax
import numpy as np

D_MODEL = 1024
BATCH = 8
SEQ = 8192
DEPTH = 4
DEC_BATCH = 32
DEC_SEQ = 64
PAST_LEN = 4096

CHUNK = 64
E_CONV = 1024
G_CONV = 8
E_SGU = 1024
G_SGU = 8
SGU_HEAD = E_SGU // G_SGU
SGU_CHUNK = 128
CONV_W = 3
MIX_WIDTH = E_CONV + E_SGU
IN_COLS = 4 * E_CONV + 3 * E_SGU
EPS = 1e-6

kernel_name = "hymba_shortconv_sgu_streaming_step"


def rmsnorm(x, g):
    xf = x.astype(jnp.float32)
    y = xf * lax.rsqrt(jnp.mean(xf * xf, axis=-1, keepdims=True) + EPS)
    return (y * g.astype(jnp.float32)).astype(x.dtype)


def layernorm(x, g, b):
    xf = x.astype(jnp.float32)
    mu = jnp.mean(xf, axis=-1, keepdims=True)
    xc = xf - mu
    var = jnp.mean(xc * xc, axis=-1, keepdims=True)
    y = xc * lax.rsqrt(var + EPS) * g.astype(jnp.float32) + b.astype(jnp.float32)
    return y.astype(x.dtype)


def short_conv(xc, prev, w):
    T = xc.shape[1]
    xp = jnp.concatenate([prev.astype(xc.dtype), xc], axis=1)
    y = w[0] * xp[:, 0:T]
    for k in range(1, CONV_W):
        y = y + w[k] * xp[:, k:k + T]
    return y, xp[:, xp.shape[1] - (CONV_W - 1):]


def spatial_gate(u, v, w_s, b_s):
    Bn, T, _ = v.shape
    n = -(-T // SGU_CHUNK)
    pad = n * SGU_CHUNK - T
    vp = jnp.pad(v, ((0, 0), (0, pad), (0, 0))).reshape(Bn, n, SGU_CHUNK, G_SGU, SGU_HEAD)
    blk = jnp.arange(SGU_CHUNK) // CHUNK
    mask = (blk[:, None] >= blk[None, :]).astype(w_s.dtype)
    mixed = jnp.einsum('gts,bnsgc->bntgc', w_s * mask, vp)
    mixed = mixed + b_s.T[None, None, :, :, None]
    mixed = mixed.reshape(Bn, n * SGU_CHUNK, E_SGU)[:, :T]
    return u * mixed


def mixer_layer(x, conv_prev, norm_g, w_in, conv_w, sg_ln_g, sg_ln_b, sg_w, sg_b,
                out_g_conv, out_g_sgu, w_out):
    xn = rmsnorm(x, norm_g)
    proj = jnp.einsum('btd,de->bte', xn, w_in)
    splits = [E_CONV, 2 * E_CONV, 3 * E_CONV, 4 * E_CONV,
              4 * E_CONV + E_SGU, 4 * E_CONV + 2 * E_SGU]
    h, bg, cg, za, u, v, zb = jnp.split(proj, splits, axis=-1)
    yc, conv_state = short_conv(cg * h, conv_prev, conv_w)
    ya = rmsnorm(bg * yc, out_g_conv) * jax.nn.silu(za)
    vn = layernorm(v, sg_ln_g, sg_ln_b)
    yb = rmsnorm(spatial_gate(u, vn, sg_w, sg_b), out_g_sgu) * jax.nn.silu(zb)
    y = jnp.einsum('bte,ed->btd', jnp.concatenate([ya, yb], axis=-1), w_out)
    return x + y, conv_state, vn


def setup_inputs(seed: int = 0) -> dict:
    key = jax.random.key(seed)
    ks = jax.random.split(key, 16)
    f32 = jnp.float32
    nrm = lambda k, shape, s: jax.random.normal(k, shape, f32) * s
    return {
        "x_prompt": nrm(ks[0], (BATCH, SEQ, D_MODEL), 1.0),
        "x_sample": nrm(ks[1], (DEC_BATCH, DEC_SEQ, D_MODEL), 1.0),
        "state_conv": nrm(ks[2], (DEPTH, DEC_BATCH, CONV_W - 1, E_CONV), 1.0),
        "norm_g": 1.0 + nrm(ks[3], (DEPTH, D_MODEL), 0.02),
        "w_in": nrm(ks[4], (DEPTH, D_MODEL, IN_COLS), D_MODEL ** -0.5),
        "conv_w": nrm(ks[5], (DEPTH, CONV_W, E_CONV), CONV_W ** -0.5),
        "sg_ln_g": 1.0 + nrm(ks[6], (DEPTH, E_SGU), 0.02),
        "sg_ln_b": nrm(ks[7], (DEPTH, E_SGU), 0.02),
        "sg_w": nrm(ks[8], (DEPTH, G_SGU, SGU_CHUNK, SGU_CHUNK), 0.5 * SGU_CHUNK ** -0.5),
        "sg_b": 1.0 + nrm(ks[9], (DEPTH, G_SGU, SGU_CHUNK), 0.1),
        "out_g_conv": 1.0 + nrm(ks[10], (DEPTH, E_CONV), 0.02),
        "out_g_sgu": 1.0 + nrm(ks[11], (DEPTH, E_SGU), 0.02),
        "w_out": nrm(ks[12], (DEPTH, MIX_WIDTH, D_MODEL), 0.5 * MIX_WIDTH ** -0.5),
        "final_g": 1.0 + nrm(ks[13], (D_MODEL,), 0.02),
    }


def reference(x_prompt, x_sample, state_conv, norm_g, w_in, conv_w, sg_ln_g, sg_ln_b,
              sg_w, sg_b, out_g_conv, out_g_sgu, w_out, final_g):
    xp = x_prompt
    zero_prev = jnp.zeros((x_prompt.shape[0], CONV_W - 1, E_CONV), x_prompt.dtype)
    conv_p = []
    for l in range(DEPTH):
        xp, cs, _ = mixer_layer(xp, zero_prev, norm_g[l], w_in[l], conv_w[l], sg_ln_g[l],
                                sg_ln_b[l], sg_w[l], sg_b[l], out_g_conv[l], out_g_sgu[l],
                                w_out[l])
        conv_p.append(cs)
    y_prompt = rmsnorm(xp, final_g)

    xs = x_sample
    conv_s, v_s = [], []
    for l in range(DEPTH):
        xs, cs, vn = mixer_layer(xs, state_conv[l], norm_g[l], w_in[l], conv_w[l], sg_ln_g[l],
                                 sg_ln_b[l], sg_w[l], sg_b[l], out_g_conv[l], out_g_sgu[l],
                                 w_out[l])
        conv_s.append(cs)
        v_s.append(vn)
    y_sample = rmsnorm(xs, final_g)

    new_conv_prompt = jnp.stack(conv_p, axis=0)
    new_conv_sample = jnp.stack(conv_s, axis=0)
    new_sgu_v_sample = jnp.stack(v_s, axis=0)
    return (y_prompt, y_sample, new_conv_prompt, new_conv_sample, new_sgu_v_sample)
```

```python
import numpy as np
from contextlib import ExitStack
import concourse.bass as bass
import concourse.mybir as mybir
from concourse.bass_utils import run_bass_kernel_spmd

F32 = mybir.dt.float32
BF16 = mybir.dt.bfloat16
AF = mybir.ActivationFunctionType
ALU = mybir.AluOpType

D = 1024
DEPTH = 4
NCOLS = 7168
EPS = 1e-6
NSLOT = 8
NUNIT = 18
N_CORES = 8
SEQ = 8192
NSEQ_S = 4
LS = 64


class Prog:
    ENGS = ("pe", "act", "dve", "pool", "sp")

    def __init__(self):
        self.ops = []
        self.lw = {}
        self.rd = {}
        self.eng_ops = {e: [] for e in self.ENGS}
        self.dma_cnt = {}
        self.debug = None

    def add(self, eng, fn, reads=(), writes=(), dma=None, ndma=1):
        i = len(self.ops)
        deps = set()
        for r in reads:
            if r in self.lw:
                deps.add(self.lw[r])
        for w in writes:
            if w in self.lw:
                deps.add(self.lw[w])
            for q in self.rd.get(w, ()):
                deps.add(q)
        deps.discard(i)
        for r in reads:
            self.rd.setdefault(r, []).append(i)
        for w in writes:
            self.lw[w] = i
            self.rd[w] = []
        op = dict(id=i, eng=eng, fn=fn, deps=deps, dma=dma, sig=None, need=False,
                  pos=len(self.eng_ops[eng]),
                  tag=",".join(list(writes)[:2]) + "<-" + ",".join(list(reads)[:3]))
        if dma is not None:
            dma = dma + ("@sw" if eng == "pool" else "@hw")
            op["dma"] = dma
            self.dma_cnt[dma] = self.dma_cnt.get(dma, 0) + ndma
            op["sig"] = (dma, 16 * self.dma_cnt[dma])
        self.ops.append(op)
        self.eng_ops[eng].append(op)
        return i

    def finalize(self):
        for op in self.ops:
            waits = []
            for d in sorted(op["deps"]):
                p = self.ops[d]
                if p["dma"] is None and p["eng"] == op["eng"]:
                    if op["eng"] == "pe":
                        continue
                    if op["pos"] - p["pos"] > 2:
                        continue
                waits.append(d)
                if p["dma"] is None:
                    p["need"] = True
            op["waits"] = waits
        for e, lst in self.eng_ops.items():
            c = 0
            for op in lst:
                if op["dma"] is None and op["need"]:
                    c += 1
                    op["sig"] = (("eng", e), c)

    def sem_names(self):
        return [("eng", e) for e in self.ENGS] + list(self.dma_cnt.keys())

    def emit(self, nc, sems):
        def run(e):
            def body(eng):
                waited = {}
                for op in self.eng_ops[e]:
                    for d in op["waits"]:
                        s, v = self.ops[d]["sig"]
                        if waited.get(s, 0) >= v:
                            continue
                        w = eng.wait_ge(sems[s], v)
                        waited[s] = v
                        if self.debug is not None:
                            self.debug["wait"][w.ins.name] = (op["id"], d)
                    r = op["fn"](eng)
                    if self.debug is not None:
                        last = r[-1] if isinstance(r, list) else r
                        self.debug["inst"][last.ins.name] = op["id"]
                    if op["dma"] is not None:
                        for ins in r:
                            ins.then_inc(sems[op["dma"]], 16)
                    elif op["need"]:
                        r.then_inc(sems[("eng", e)], 1)
                if e == "sp":
                    for s, c in self.dma_cnt.items():
                        eng.wait_ge(sems[s], 16 * c)
            return body

        with nc.Block() as block:
            block.tensor(run("pe"))
            block.scalar(run("act"))
            block.vector(run("dve"))
            block.gpsimd(run("pool"))
            block.sync(run("sp"))


def build_nc(NT=16, debug=None):
    nc = bass.Bass("TRN2", target_bir_lowering=False)
    SEQP = NT * 512

    def din(name, shape):
        return nc.dram_tensor(name, list(shape), F32, kind="ExternalInput").ap()

    def dout(name, shape):
        return nc.dram_tensor(name, list(shape), F32, kind="ExternalOutput").ap()

    xp = din("xp", [SEQP, D])
    xs = din("xs", [NSEQ_S * LS, D])
    sc = din("sc", [DEPTH, NSEQ_S, 2, D])
    ngd = din("norm_g", [DEPTH, D])
    win = din("w_in", [DEPTH, D, NCOLS])
    cwd = din("conv_w", [DEPTH, 3, D])
    lngd = din("sg_ln_g", [DEPTH, D])
    lnbd = din("sg_ln_b", [DEPTH, D])
    sgwd = din("sg_w", [DEPTH, 8, 128, 128])
    sgbd = din("sg_b", [DEPTH, 8, 128])
    ogad = din("out_g_conv", [DEPTH, D])
    ogbd = din("out_g_sgu", [DEPTH, D])
    woutd = din("w_out", [DEPTH, 2048, D])
    fgd = din("final_g", [D])

    yp = dout("yp", [SEQP, D])
    ys = dout("ys", [NSEQ_S * LS, D])
    ncp = dout("ncp", [DEPTH, 2, D])
    ncs = dout("ncs", [DEPTH, NSEQ_S * 2, D])
    nvo = dout("nv", [DEPTH, NSEQ_S * LS, D])

    wsc = nc.dram_tensor("wsc", [DEPTH, NUNIT, 128, 4096], BF16, kind="Internal").ap()
    stf = nc.dram_tensor("stf", [DEPTH, 128, 1024], F32, kind="Internal").ap()

    base0 = (nc.sbuf_base + 63) // 64 * 64
    cur = [base0]

    def sb(name, shape, dtype, at=None):
        nbytes = int(np.prod(shape[1:])) * (2 if dtype == BF16 else 4)
        if at is None:
            off = cur[0]
            cur[0] += (nbytes + 63) // 64 * 64
        else:
            off = at
        t = nc.alloc_sbuf_tensor_at(name, list(shape), dtype, offset=off)
        return t, off

    NF = 12
    ident_t, _ = sb("ident", [128, 128], F32)
    ones_t, _ = sb("ones", [128, 128], BF16)
    mh_t, _ = sb("mh", [128, 8], F32)
    cst_t, _ = sb("cst", [128, 232], F32)
    stg_t, _ = sb("stg", [128, 2, 128], F32)
    pst_t, _ = sb("pst", [128, DEPTH, 8, 2], F32)
    sst_t, _ = sb("sst", [128, DEPTH * 8 * 8], F32)
    snew_t, _ = sb("snew", [128, DEPTH, 8, 8], F32)
    ostg_t, _ = sb("ostg", [128, 1024], F32)
    x_t, x_off = sb("xres", [128, 8, 512], F32)
    xn_t, xn_off = sb("xn", [128, 8, 512], BF16)
    vn_t, vn_off = sb("vn", [128, 4, 1024], BF16)
    yc_t, yc_off = sb("ycat", [128, 16, 512], BF16)
    sq_t, _ = sb("sq", [128, 6, 512], BF16)
    ft_t, _ = sb("ft", [128, NF, 516], F32)
    rst_t, _ = sb("rst", [128, 3, 512], F32)
    wr_t, _ = sb("wring", [128, NSLOT, 4096], BF16)
    wmt_t, _ = sb("wmt", [128, DEPTH, 2, 8, 128], BF16)
    gb_t, _ = sb("gb", [128, 3072], F32)
    sgs_t, _ = sb("sgst", [128, 2, 8, 128], F32)
    bnst_t, _ = sb("bnst", [128, 4, 2, 6], F32)
    mv_t, _ = sb("mv", [128, 4, 4], F32)
    vstg_t, _ = sb("vstg", [128, 2, 512], F32)
    junk_t, _ = sb("junk", [128, 512], BF16)
    xnp_t, _ = sb("xnp", [128, 8, 512], BF16)
    dg_t, _ = sb("dg", [128, 4, 128], BF16)
    epv_t, _ = sb("epv", [128, 4], F32)
    assert xn_off + 8192 == vn_off
    xin_t, _ = sb("xin", [128, 4, 1024], F32, at=xn_off)
    xout_t, _ = sb("xout", [128, 4, 1024], F32, at=yc_off)
    sstg_t, _ = sb("sstg", [128, DEPTH, 1024], F32, at=yc_off)
    assert cur[0] <= nc.sbuf_top, (cur[0], nc.sbuf_top)
    nc.alloc_sbuf_tensor("arena", [128, (cur[0] - nc.sbuf_base + 3) // 4], F32)

    IDENT = ident_t[:]
    ONES = ones_t[:]
    MH = mh_t[:]
    CST = cst_t[:]
    STG = stg_t[:]
    PST = pst_t[:]
    SST = sst_t[:]
    SNEW = snew_t[:]
    OSTG = ostg_t[:]
    X = x_t[:]
    XN = xn_t[:]
    VN = vn_t[:]
    YC = yc_t[:]
    SQ = sq_t[:]
    FT = ft_t[:]
    RST = rst_t[:]
    WR = wr_t[:]
    WMT = wmt_t[:]
    GB = gb_t[:]
    BNST = bnst_t[:]
    MV = mv_t[:]
    VSTG = vstg_t[:]
    JUNK = junk_t[:]
    XNP = xnp_t[:]
    DG = dg_t[:]
    EPV = epv_t[:]
    XIN = xin_t[:]
    XOUT = xout_t[:]
    SG = sgs_t[:]
    SSTG = sstg_t[:]

    banks = [nc.alloc_psum_tensor(f"bank{b}", [128, 512], F32)[:] for b in range(8)]

    P = Prog()
    if debug is not None:
        debug["wait"] = {}
        debug["inst"] = {}
        debug["P"] = P
        P.debug = debug

    XALL = [f"x{k}" for k in range(8)]
    YCALL = [f"yc{i}" for i in range(16)]
    XIN_RES = [[f"xn{k}" for k in range(0, 4)], [f"xn{k}" for k in range(4, 8)],
               ["vn0", "vn1"], ["vn2", "vn3"]]
    XOUT_RES = [[f"yc{i}" for i in range(4 * tb, 4 * tb + 4)] for tb in range(4)]

    rot = {"ps": 0, "f": 0, "sq": 0, "vst": 0}

    def nb():
        b = rot["ps"]
        rot["ps"] = (b + 1) % 6
        return b

    def nf():
        i = rot["f"]
        rot["f"] = (i + 1) % NF
        return i

    def nsq():
        i = rot["sq"]
        rot["sq"] = (i + 1) % 6
        return i

    def cc(col):
        return CST[:, col:col + 1]

    def p0_ident(eng):
        return eng.affine_select(out=IDENT, in_=IDENT, pattern=[[-1, 128]], compare_op=ALU.not_equal,
                                 fill=1.0, base=0, channel_multiplier=1)

    UNIT_COL = [5120, 5632, 0, 1024, 2048, 3072, 512, 1536, 2560, 3584, 4096, 6144, 4608, 6656]

    def cast_unit(eng, l, u):
        if u < 14:
            c0 = UNIT_COL[u]
            return eng.dma_start(out=wsc[l, u].rearrange("p (k c) -> p k c", k=8),
                                 in_=win[l, :, c0:c0 + 512].rearrange("(k p) c -> p k c", p=128))
        mp = u - 14
        return eng.dma_start(out=wsc[l, u].rearrange("p (k c) -> p k c", k=16),
                             in_=woutd[l, :, mp * 256:(mp + 1) * 256].rearrange("(k p) c -> p k c", p=128))

    cast_pending = []
    cast_ops = {}

    def emit_cast0(units, l=0):
        for u in units:
            P.add("pool", lambda eng, u=u, l=l: [cast_unit(eng, l, u)], writes=[f"wsc{l}_{u}"], dma=f"c{l}_{u}")

    CAST_GROUP = 5

    def emit_cast(l):
        groups = {}
        for u in range(NUNIT):
            grp = u // CAST_GROUP
            i = P.add("pool", lambda eng, l=l, u=u: [cast_unit(eng, l, u)], writes=[f"wsc{l}_{u}"], dma=f"cast{l}g{grp}")
            groups.setdefault(grp, []).append(i)
        for grp, lst in groups.items():
            for j in lst:
                P.ops[j]["sig"] = (P.ops[j]["dma"], 16 * len(lst))
        cast_ops[l] = True

    def p0_const_load(eng):
        r = []
        r.append(eng.dma_start(out=STG[0:32, 0, :], in_=ngd.rearrange("l (k c) -> (l k) c", c=128)))
        r.append(eng.dma_start(out=STG[32:128, 0, :], in_=cwd.rearrange("l r (k c) -> (l r k) c", c=128)))
        r.append(eng.dma_start(out=STG[0:32, 1, :], in_=ogad.rearrange("l (k c) -> (l k) c", c=128)))
        r.append(eng.dma_start(out=STG[32:64, 1, :], in_=ogbd.rearrange("l (k c) -> (l k) c", c=128)))
        r.append(eng.dma_start(out=STG[64:72, 1, :], in_=fgd.rearrange("(k c) -> k c", c=128)))
        r.append(eng.dma_start(out=STG[72:104, 1, :], in_=lngd.rearrange("l (k c) -> (l k) c", c=128)))
        return r

    def p0_const_tr(eng):
        eng.transpose(banks[0][:, 0:128], STG[:, 0, :], IDENT)
        return eng.transpose(banks[0][:, 128:232], STG[0:104, 1, :], IDENT[0:104, 0:104])

    def sst_prep():
        P.add("sp", lambda eng: [eng.dma_start(out=SSTG[0:8, :, :], in_=sc.rearrange("l s r c -> (s r) l c"))],
              writes=YCALL, dma="scload")
        b = nb()

        def p0_sst_tr(eng):
            ins = None
            for l in range(DEPTH):
                for j in range(8):
                    c0 = (l * 8 + j) * 8
                    ins = eng.transpose(banks[b][:, c0:c0 + 8], SSTG[0:8, l, j * 128:(j + 1) * 128], IDENT[0:8, 0:8])
            return ins
        P.add("pe", p0_sst_tr, reads=YCALL + ["ident"], writes=[f"ps{b}"])
        P.add("act", lambda eng: eng.activation(out=SST, in_=banks[b][:, 0:256], func=AF.Copy),
              reads=[f"ps{b}"], writes=["sst"])

    ldq = ["act"]

    def sg_prep_load(l, first_tile=False):
        P.add("dve", lambda eng: eng.memset(SG[:, 1, :, :], 0.0), writes=["sg"])

        def sg_load(eng, l=l):
            r = []
            r.append(eng.dma_start(out=SG[:, 0, :, :], in_=sgwd[l].rearrange("g t s -> t g s")))
            r.append(eng.dma_start(out=SG[0:64, 1, :, 0:64], in_=sgwd[l, :, 0:64, 0:64].rearrange("g t s -> t g s")))
            r.append(eng.dma_start(out=SG[64:128, 1, :, 64:128], in_=sgwd[l, :, 0:64, 0:64].rearrange("g t s -> t g s")))
            return r
        P.add(ldq[0], sg_load, writes=["sg"], dma="sgload", ndma=3)

    def sg_prep_pe(l):
        for v in range(2):
            for gh in range(2):
                b = nb()

                def sg_tr(eng, v=v, gh=gh, b=b):
                    ins = None
                    for gi in range(4):
                        ins = eng.transpose(banks[b][:, gi * 128:(gi + 1) * 128], SG[:, v, gh * 4 + gi, :], IDENT)
                    return ins
                P.add("pe", sg_tr, reads=["sg", "ident"], writes=[f"ps{b}"])
                P.add("act", lambda eng, l=l, v=v, gh=gh, b=b: eng.activation(
                    out=WMT[:, l, v, gh * 4:(gh + 1) * 4, :],
                    in_=banks[b][:, 0:512].rearrange("p (g t) -> p g t", g=4), func=AF.Copy),
                    reads=[f"ps{b}"], writes=[f"wmt{l}"])
        P.add("dve", lambda eng, l=l: eng.memset(WMT[64:128, l, 0, :, 0:64], 0.0),
              reads=[f"wmt{l}"], writes=[f"wmt{l}"])

    def stuff_prep(l):
        fb = [nf(), nf()]
        fc2 = [nf(), nf()]
        sqs = [nsq(), nsq()]

        def ld(eng):
            r = []
            for h in range(2):
                r.append(eng.dma_start(out=FT[:, fb[h], 0:512], in_=lnbd[l:l + 1, h * 512:(h + 1) * 512].broadcast_to([128, 512])))
                r.append(eng.dma_start(out=FT[:, fc2[h], 0:512].rearrange("p (g t) -> p g t", g=4),
                                       in_=sgbd[l:l + 1, h * 4:(h + 1) * 4, :].broadcast_to([128, 4, 128])))
            return r
        P.add(ldq[0], ld, writes=[f"f{x}" for x in fb + fc2], dma=f"stfl{l}", ndma=4)
        for h in range(2):
            P.add("act", lambda eng, h=h: eng.activation(out=SQ[:, sqs[h], 0:512], in_=FT[:, fb[h], 0:512], func=AF.Copy),
                  reads=[f"f{fb[h]}"], writes=[f"sq{sqs[h]}"])
            b = nb()

            def st_mm(eng, h=h, b=b):
                ins = None
                for gi in range(4):
                    ins = eng.matmul(banks[b][:, gi * 128:(gi + 1) * 128], SQ[:, sqs[h], gi * 128:(gi + 1) * 128],
                                     WMT[:, l, 0, h * 4 + gi, :], start=True, stop=True)
                return ins
            P.add("pe", st_mm, reads=[f"sq{sqs[h]}", f"wmt{l}"], writes=[f"ps{b}"])
            P.add("dve", lambda eng, h=h, b=b: eng.tensor_tensor(out=FT[:, fc2[h], 0:512], in0=banks[b][:, 0:512],
                                                                 in1=FT[:, fc2[h], 0:512], op=ALU.add),
                  reads=[f"ps{b}", f"f{fc2[h]}"], writes=[f"f{fc2[h]}"])
        P.add("sp", lambda eng: [eng.dma_start(out=stf[l, :, h * 512:(h + 1) * 512], in_=FT[:, fc2[h], 0:512]) for h in range(2)],
              reads=[f"f{x}" for x in fc2], writes=[f"stf{l}"], dma=f"stfs{l}", ndma=2)

    tiles = [("p", ti) for ti in range(NT)] + [("s", 0)]
    NTL = len(tiles) * DEPTH
    total_units = NTL * NUNIT
    loaded = [0]

    def emit_wload(n):
        l = (n // NUNIT) % DEPTH
        u = n % NUNIT
        s = n % NSLOT
        P.add("sp", lambda eng, l=l, u=u, s=s: [eng.dma_start(out=WR[:, s, :], in_=wsc[l, u])],
              reads=[f"wsc{l}_{u}"], writes=[f"w{s}"], dma=f"wl{s}")
        assert l <= 1 or cast_ops.get(l)

    def release_unit(n):
        nn = n + NSLOT
        if nn < total_units:
            emit_wload(nn)

    def emit_gbA(tl):
        if tl >= NTL or tiles[tl // DEPTH][0] == "p":
            return
        l = tl % DEPTH
        P.add(ldq[0], lambda eng: [
            eng.dma_start(out=GB[:, 0:1024], in_=lngd[l:l + 1, :].broadcast_to([128, 1024])),
            eng.dma_start(out=GB[:, 1024:2048], in_=lnbd[l:l + 1, :].broadcast_to([128, 1024]))],
            writes=["gbA"], dma="gblA", ndma=2)

    def emit_gbB(tl):
        if tl >= NTL:
            return
        kind = tiles[tl // DEPTH][0]
        l = tl % DEPTH

        def fn(eng):
            r = []
            if kind == "p":
                r.append(eng.dma_start(out=GB[:, 2048:3072], in_=stf[l]))
            else:
                gv = GB[:, 2048:3072].rearrange("p (g r t) -> p g r t", g=8, r=2)
                for rr in range(2):
                    r.append(eng.dma_start(out=gv[:, :, rr, :], in_=sgbd[l:l + 1, :, 0:64].broadcast_to([128, 8, 64])))
            return r
        P.add(ldq[0], fn, reads=([f"stf{l}"] if kind == "p" else []), writes=["gbB"], dma="gblB", ndma=(1 if kind == "p" else 2))

    def emit_xin_load(tidx):
        if tidx >= len(tiles):
            return
        kind, ti = tiles[tidx]
        if kind == "p":
            src = xp[ti * 512:(ti + 1) * 512, :].rearrange("(tb p) d -> p tb d", p=128)
            ntb = 4
        else:
            src = xs.rearrange("(tb p) d -> p tb d", p=128)
            ntb = 2
        res = []
        for tb in range(ntb):
            res += XIN_RES[tb]
        P.add(ldq[0], lambda eng: [eng.dma_start(out=XIN[:, 0:ntb, :], in_=src)], writes=res, dma="xinl")

    emit_xin_load(0)
    emit_cast0([0, 1])
    P.add("pool", lambda eng: eng.memset(IDENT, 0.0), writes=["ident"])
    P.add("pool", p0_ident, reads=["ident"], writes=["ident"])
    P.add("pool", lambda eng: eng.memset(ONES, 1.0 / 1024.0), writes=["ones"])
    P.add("pool", lambda eng: eng.memset(MH, -0.5), writes=["mh"])
    P.add("pool", lambda eng: eng.memset(PST, 0.0), writes=[f"pst{l}_{j}" for l in range(DEPTH) for j in range(8)])
    P.add("act", p0_const_load, writes=["stg"], dma="cload", ndma=6)
    sg_prep_load(0)
    emit_cast0(range(2, NUNIT))
    emit_cast0(range(NUNIT), l=1)
    for l in range(2, DEPTH):
        emit_cast(l)
    P.add("pe", p0_const_tr, reads=["stg", "ident"], writes=["ps0"])
    P.add("act", lambda eng: eng.activation(out=CST, in_=banks[0][:, 0:232], func=AF.Copy),
          reads=["ps0"], writes=["cst"])
    rot["ps"] = 1
    for n in range(min(NSLOT, total_units)):
        emit_wload(n)
    sg_prep_pe(0)
    stuff_prep(0)
    emit_gbA(0)
    emit_gbB(0)

    def prologue(tidx):
        kind, ti = tiles[tidx]
        T = 512 if kind == "p" else 256
        NTB = T // 128
        pend = []

        def flush_stats(bankno, n_total, keep):
            while len(pend) > keep:
                sqi, idx = pend.pop(0)
                P.add("pe", lambda eng, sqi=sqi, idx=idx: eng.matmul(
                    banks[bankno][:, 0:T], ONES, SQ[:, sqi, 0:T], start=(idx == 0), stop=(idx == n_total - 1)),
                    reads=[f"sq{sqi}", "ones"], writes=[f"ps{bankno}"])

        for k in range(8):
            b = nb()

            def tr_in(eng, k=k, b=b):
                ins = None
                for tb in range(NTB):
                    ins = eng.transpose(banks[b][:, tb * 128:(tb + 1) * 128], XIN[:, tb, k * 128:(k + 1) * 128], IDENT)
                return ins
            rres = []
            for tb in range(NTB):
                rres += XIN_RES[tb]
            P.add("pe", tr_in, reads=rres + ["ident"], writes=[f"ps{b}"])
            P.add("act", lambda eng, k=k, b=b: eng.activation(out=XNP[:, k, 0:T], in_=banks[b][:, 0:T], func=AF.Identity, scale=cc(k)),
                  reads=[f"ps{b}", "cst"], writes=[f"xnp{k}"])
            sqi = nsq()
            P.add("act", lambda eng, b=b, sqi=sqi: eng.activation(out=SQ[:, sqi, 0:T], in_=banks[b][:, 0:T], func=AF.Square),
                  reads=[f"ps{b}"], writes=[f"sq{sqi}"])
            P.add("act", lambda eng, k=k, b=b: eng.activation(out=X[:, k, 0:T], in_=banks[b][:, 0:T], func=AF.Copy),
                  reads=[f"ps{b}"], writes=[f"x{k}"])
            pend.append((sqi, k))
            flush_stats(6, 8, 1)
        flush_stats(6, 8, 0)

    def do_tile(tidx):
        kind, ti = tiles[tidx]
        T = 512 if kind == "p" else 256
        NTB = T // 128
        nseg = 1 if kind == "p" else NSEQ_S
        L = T // nseg
        kv = 0 if kind == "p" else 1
        pe_ = "dve" if tidx == 0 else "pool"
        ldq[0] = "act" if tidx == 0 else "pool"

        def seg(ap):
            return ap.rearrange("p (s c) -> p s c", s=nseg)

        def fbuf(i):
            return FT[:, i, 0:T]

        class Stat:
            def __init__(self, bank):
                self.bank, self.raw, self.ready, self.idx = bank, [], [], 0

            def add(self, sqi):
                self.raw.append(sqi)

            def pairs(self, hold=0):
                while len(self.raw) - hold >= 2:
                    a, b2 = self.raw.pop(0), self.raw.pop(0)
                    sn = nsq()
                    P.add("dve", lambda eng, a=a, b2=b2, sn=sn: eng.tensor_tensor(out=SQ[:, sn, 0:T], in0=SQ[:, a, 0:T], in1=SQ[:, b2, 0:T], op=ALU.add),
                          reads=[f"sq{a}", f"sq{b2}"], writes=[f"sq{sn}"])
                    self.ready.append((sn, self.idx))
                    self.idx += 1

            def flush(self, keep=0):
                while len(self.ready) > keep:
                    sqi, idx = self.ready.pop(0)
                    bankno = self.bank
                    P.add("pe", lambda eng, sqi=sqi, idx=idx, bankno=bankno: eng.matmul(
                        banks[bankno][:, 0:T], ONES, SQ[:, sqi, 0:T], start=(idx == 0), stop=(idx == 3)),
                        reads=[f"sq{sqi}", "ones"], writes=[f"ps{bankno}"])

            def drain(self):
                self.pairs()
                self.flush(0)
                assert self.idx == 4 and not self.raw

        carry = [None]
        for l in range(DEPTH):
            tl = tidx * DEPTH + l
            par = tl % 2
            ubase = tl * NUNIT

            def emit_rx():
                if carry[0] is not None:
                    carry[0].drain()
                    carry[0] = None
                P.add("act", lambda eng: eng.activation(out=RST[:, 0, 0:T], in_=banks[6][:, 0:T], func=AF.Ln, bias=EPS),
                      reads=["ps6"], writes=["rx"])
                P.add("act", lambda eng: eng.activation(out=RST[:, 0, 0:T], in_=RST[:, 0, 0:T], func=AF.Exp, scale=-0.5),
                      reads=["rx"], writes=["rx"])

            def emit_epv():
                for tb in range(NTB):
                    P.add("dve", lambda eng, tb=tb: eng.tensor_tensor(out=DG[:, tb, :], in0=RST[:, 0, tb * 128:(tb + 1) * 128], in1=IDENT, op=ALU.mult),
                          reads=["rx", "ident"], writes=[f"dg{tb}"])

            def emit_epv2():
                be = nb()

                def ep_mm(eng, be=be):
                    ins = None
                    for tb in range(NTB):
                        ins = eng.matmul(banks[be][:, tb:tb + 1], DG[:, tb, :], ONES[:, 0:1], start=True, stop=True)
                    return ins
                P.add("pe", ep_mm, reads=[f"dg{tb}" for tb in range(NTB)] + ["ones"], writes=[f"ps{be}"])
                fe = nf()
                P.add("act", lambda eng, be=be, fe=fe: eng.activation(out=FT[:, fe, 0:NTB], in_=banks[be][:, 0:NTB], func=AF.Copy),
                      reads=[f"ps{be}"], writes=[f"f{fe}"])
                P.add("dve", lambda eng, fe=fe: eng.scalar_tensor_tensor(
                    out=EPV[:, 0:NTB], in0=FT[:, fe, 0:NTB], scalar=1024.0 * 1024.0 / EPS, in1=FT[:, fe, 0:NTB],
                    op0=ALU.mult, op1=ALU.mult), reads=[f"f{fe}"], writes=["epv"])
                P.add("dve", lambda eng: eng.reciprocal(EPV[:, 0:NTB], EPV[:, 0:NTB]), reads=["epv"], writes=["epv"])

            def emit_xn():
                for k in range(8):
                    P.add("dve", lambda eng, k=k, l=l: eng.scalar_tensor_tensor(
                        out=XN[:, k, 0:T], in0=X[:, k, 0:T], scalar=cc(l * 8 + k), in1=RST[:, 0, 0:T],
                        op0=ALU.mult, op1=ALU.mult), reads=[f"x{k}", "rx", "cst"], writes=[f"xn{k}"])


            if tidx == 0 and l + 1 < DEPTH:
                sg_prep_load(l + 1, first_tile=True)

            vfis = {}

            def v_post2A(tbp, vbank):
                fis = {}
                sres = []
                for tb in (tbp, tbp + 1):
                    for half in range(2):
                        b = vbank[(tb, half)]
                        fi = nf()
                        fis[(tb, half)] = fi
                        P.add("act", lambda eng, b=b, fi=fi, tb=tb, half=half: eng.activation(
                            out=FT[:, fi, 0:512], in_=banks[b][:, 0:512], func=AF.Copy, accum_out=BNST[:, tb, half, 0:1]),
                            reads=[f"ps{b}"], writes=[f"f{fi}", f"bnst{tb}_{half}a"])
                for tb in (tbp, tbp + 1):
                    for half in range(2):
                        fi = fis[(tb, half)]
                        P.add("act", lambda eng, fi=fi, tb=tb, half=half: eng.activation(
                            out=JUNK, in_=FT[:, fi, 0:512], func=AF.Square, accum_out=BNST[:, tb, half, 1:2]),
                            reads=[f"f{fi}"], writes=[f"bnst{tb}_{half}b"])
                        sres += [f"bnst{tb}_{half}a", f"bnst{tb}_{half}b"]
                vfis[tbp] = (fis, sres)

            def v_post2B(tbp):
                fis, sres = vfis[tbp]
                mvr = f"mv{tbp}"
                M2 = MV[:, tbp:tbp + 2, :]
                P.add("dve", lambda eng: eng.tensor_tensor(out=M2[:, :, 0:2], in0=BNST[:, tbp:tbp + 2, 0, 0:2],
                                                           in1=BNST[:, tbp:tbp + 2, 1, 0:2], op=ALU.add),
                      reads=sres, writes=[mvr])
                P.add("dve", lambda eng: eng.scalar_tensor_tensor(
                    out=M2[:, :, 2:3], in0=M2[:, :, 0:1], scalar=1.0 / (1024.0 * 1024.0), in1=M2[:, :, 0:1], op0=ALU.mult, op1=ALU.mult),
                    reads=[mvr], writes=[mvr])
                P.add("dve", lambda eng: eng.scalar_tensor_tensor(
                    out=M2[:, :, 1:2], in0=M2[:, :, 1:2], scalar=1.0 / 1024.0, in1=M2[:, :, 2:3], op0=ALU.mult, op1=ALU.subtract),
                    reads=[mvr], writes=[mvr])
                P.add("dve", lambda eng: eng.scalar_tensor_tensor(out=M2[:, :, 2:3], in0=M2[:, :, 1:2], scalar=0.0,
                                                                  in1=EPV[:, tbp:tbp + 2].unsqueeze(2), op0=ALU.max, op1=ALU.add),
                      reads=[mvr, "epv"], writes=[mvr])
                P.add("act", lambda eng: eng.activation(out=M2[:, :, 2:3], in_=M2[:, :, 2:3], func=AF.Ln),
                      reads=[mvr], writes=[mvr])
                P.add("act", lambda eng: eng.activation(out=M2[:, :, 2:3], in_=M2[:, :, 2:3], func=AF.Exp, scale=-0.5),
                      reads=[mvr], writes=[mvr])
                P.add("dve", lambda eng: eng.scalar_tensor_tensor(
                    out=M2[:, :, 3:4], in0=M2[:, :, 0:1], scalar=-1.0 / 1024.0, in1=M2[:, :, 2:3], op0=ALU.mult, op1=ALU.mult),
                    reads=[mvr], writes=[mvr])
                for tb in (tbp, tbp + 1):
                    for half in range(2):
                        fi = fis[(tb, half)]
                        if kind == "p":
                            P.add(pe_, lambda eng, tb=tb, fi=fi, half=half: eng.tensor_scalar(
                                out=VN[:, tb, half * 512:(half + 1) * 512], in0=FT[:, fi, 0:512], scalar1=MV[:, tb, 2:3],
                                scalar2=MV[:, tb, 3:4], op0=ALU.mult, op1=ALU.add),
                                reads=[f"f{fi}", mvr], writes=[f"vn{tb}"])
                            continue
                        P.add(pe_, lambda eng, tb=tb, fi=fi: eng.tensor_scalar(
                            out=FT[:, fi, 0:512], in0=FT[:, fi, 0:512], scalar1=MV[:, tb, 2:3], scalar2=MV[:, tb, 3:4],
                            op0=ALU.mult, op1=ALU.add),
                            reads=[f"f{fi}", mvr], writes=[f"f{fi}"])
                        P.add(pe_, lambda eng, fi=fi, half=half, par=par: eng.tensor_tensor(
                            out=FT[:, fi, 0:512], in0=FT[:, fi, 0:512], in1=GB[:, half * 512:(half + 1) * 512], op=ALU.mult),
                            reads=[f"f{fi}", "gbA"], writes=[f"f{fi}"])
                        if kind == "p":
                            P.add(pe_, lambda eng, fi=fi, half=half, par=par, tb=tb: eng.tensor_tensor(
                                out=VN[:, tb, half * 512:(half + 1) * 512], in0=FT[:, fi, 0:512],
                                in1=GB[:, 1024 + half * 512:1024 + (half + 1) * 512], op=ALU.add),
                                reads=[f"f{fi}", "gbA"], writes=[f"vn{tb}"])
                        else:
                            vi = rot["vst"]
                            rot["vst"] = 1 - vi
                            P.add(pe_, lambda eng, fi=fi, half=half, par=par, vi=vi: eng.tensor_tensor(
                                out=VSTG[:, vi, :], in0=FT[:, fi, 0:512],
                                in1=GB[:, 1024 + half * 512:1024 + (half + 1) * 512], op=ALU.add),
                                reads=[f"f{fi}", "gbA"], writes=[f"vst{vi}"])
                            P.add("act", lambda eng, vi=vi, tb=tb, half=half: eng.activation(
                                out=VN[:, tb, half * 512:(half + 1) * 512], in_=VSTG[:, vi, :], func=AF.Copy),
                                reads=[f"vst{vi}"], writes=[f"vn{tb}"])
                            P.add("sp", lambda eng, vi=vi, tb=tb, half=half, l=l: [eng.dma_start(
                                out=nvo[l, tb * 128:(tb + 1) * 128, half * 512:(half + 1) * 512], in_=VSTG[:, vi, :])],
                                reads=[f"vst{vi}"], dma=f"vsto{vi}")

            vbank = {}
            for tbp in range(0, NTB, 2):
                bm4 = {(tb, half): nb() for tb in (tbp, tbp + 1) for half in range(2)}
                vbank.update(bm4)
                sl = [(ubase + half) % NSLOT for half in range(2)]
                if tbp == 0:
                    for k in range(8):
                        def v_mm(eng, k=k, bm4=bm4, sl=sl):
                            ins = None
                            for (tb, half), b in bm4.items():
                                wv = WR[:, sl[half], :].rearrange("p (k c) -> p k c", k=8)
                                ins = eng.matmul(banks[b][:, 0:512], XNP[:, k, tb * 128:(tb + 1) * 128], wv[:, k, :],
                                                 start=(k == 0), stop=(k == 7))
                            return ins
                        P.add("pe", v_mm, reads=[f"xnp{k}"] + [f"w{x}" for x in sl], writes=[f"ps{b}" for b in bm4.values()])
                else:
                    for (tb, half), b in bm4.items():
                        def v_mm2(eng, tb=tb, half=half, b=b, sl=sl):
                            wv = WR[:, sl[half], :].rearrange("p (k c) -> p k c", k=8)
                            ins = None
                            for k in range(8):
                                ins = eng.matmul(banks[b][:, 0:512], XNP[:, k, tb * 128:(tb + 1) * 128], wv[:, k, :],
                                                 start=(k == 0), stop=(k == 7))
                            return ins
                        P.add("pe", v_mm2, reads=[f"xnp{k}" for k in range(8)] + [f"w{sl[half]}"], writes=[f"ps{b}"])
                if tbp + 2 >= NTB:
                    release_unit(ubase)
                    release_unit(ubase + 1)
                if tbp == 0:
                    emit_rx()
                    emit_epv()
                    emit_xn()
                v_post2A(tbp, vbank)
                if tbp + 2 >= NTB:
                    emit_epv2()
                    for t2 in range(0, NTB, 2):
                        v_post2B(t2)
            emit_gbA(tl + 1)
            stA = Stat(6)
            for j in range(8):
                jj = j % 4
                bs = []
                bs = [None] * 4
                for si in (0, 2, 3, 1):
                    n = ubase + 2 + (j // 4) * 4 + si
                    s = n % NSLOT
                    b = nb()
                    bs[si] = b

                    def a_mm(eng, s=s, jj=jj, b=b):
                        wv = WR[:, s, :].rearrange("p (k c) -> p k c", k=8)
                        ins = None
                        for k in range(8):
                            ins = eng.matmul(banks[b][:, 0:T], wv[:, k, jj * 128:(jj + 1) * 128], XN[:, k, 0:T],
                                             start=(k == 0), stop=(k == 7))
                        return ins
                    P.add("pe", a_mm, reads=[f"xn{k}" for k in range(8)] + [f"w{s}"], writes=[f"ps{b}"])
                if jj == 3:
                    for si in range(4):
                        release_unit(ubase + 2 + (j // 4) * 4 + si)
                stA.pairs()
                stA.flush(1)
                bh, bb, bc, bz = bs
                fh = nf()
                P.add("act", lambda eng, bh=bh, fh=fh: eng.activation(out=fbuf(fh), in_=banks[bh][:, 0:T], func=AF.Copy),
                      reads=[f"ps{bh}"], writes=[f"f{fh}"])
                fz = nf()
                P.add("act", lambda eng, bz=bz, fz=fz: eng.activation(out=fbuf(fz), in_=banks[bz][:, 0:T], func=AF.Silu),
                      reads=[f"ps{bz}"], writes=[f"f{fz}"])
                fc = nf()
                CH = FT[:, fc, 0:nseg * (L + 2)].rearrange("p (s c) -> p s c", s=nseg)
                if kind == "p":
                    hist = PST[:, l, j, :].unsqueeze(1)
                    hres = f"pst{l}_{j}"
                else:
                    c0 = (l * 8 + j) * 8
                    hist = SST[:, c0:c0 + 8].rearrange("p (s r) -> p s r", s=nseg)
                    hres = "sst"
                P.add("dve", lambda eng, CH=CH, hist=hist: eng.tensor_copy(CH[:, :, 0:2], hist),
                      reads=[hres], writes=[f"f{fc}"])
                P.add("dve", lambda eng, CH=CH, bc=bc, fh=fh: eng.tensor_tensor(
                    out=CH[:, :, 2:2 + L], in0=seg(banks[bc][:, 0:T]), in1=seg(fbuf(fh)), op=ALU.mult),
                    reads=[f"ps{bc}", f"f{fh}"], writes=[f"f{fc}"])
                if kind == "p":
                    P.add("dve", lambda eng, CH=CH, l=l, j=j: eng.tensor_copy(PST[:, l, j, :].unsqueeze(1), CH[:, :, L:L + 2]),
                          reads=[f"f{fc}"], writes=[f"pst{l}_{j}"])
                else:
                    P.add("dve", lambda eng, CH=CH, l=l, j=j: eng.tensor_copy(
                        SNEW[:, l, j, :].rearrange("p (s r) -> p s r", s=nseg), CH[:, :, L:L + 2]),
                        reads=[f"f{fc}"], writes=[f"snew{l}_{j}"])
                fa = nf()
                ACC = seg(fbuf(fa))
                wcol = 32 + (l * 3) * 8 + j
                P.add("dve", lambda eng, CH=CH, ACC=ACC, wcol=wcol: eng.tensor_scalar(
                    out=ACC, in0=CH[:, :, 2:2 + L], scalar1=cc(wcol + 16), scalar2=None, op0=ALU.mult),
                    reads=[f"f{fc}", "cst"], writes=[f"f{fa}"])
                P.add("dve", lambda eng, CH=CH, ACC=ACC, wcol=wcol: eng.scalar_tensor_tensor(
                    out=ACC, in0=CH[:, :, 1:1 + L], scalar=cc(wcol + 8), in1=ACC, op0=ALU.mult, op1=ALU.add),
                    reads=[f"f{fc}", f"f{fa}", "cst"], writes=[f"f{fa}"])
                P.add("dve", lambda eng, CH=CH, ACC=ACC, wcol=wcol: eng.scalar_tensor_tensor(
                    out=ACC, in0=CH[:, :, 0:L], scalar=cc(wcol), in1=ACC, op0=ALU.mult, op1=ALU.add),
                    reads=[f"f{fc}", f"f{fa}", "cst"], writes=[f"f{fa}"])
                P.add("dve", lambda eng, bb=bb, fa=fa: eng.tensor_tensor(out=fbuf(fa), in0=banks[bb][:, 0:T], in1=fbuf(fa), op=ALU.mult),
                      reads=[f"ps{bb}", f"f{fa}"], writes=[f"f{fa}"])
                sqi = nsq()
                P.add("act", lambda eng, fa=fa, sqi=sqi: eng.activation(out=SQ[:, sqi, 0:T], in_=fbuf(fa), func=AF.Square),
                      reads=[f"f{fa}"], writes=[f"sq{sqi}"])
                P.add("dve", lambda eng, fa=fa, fz=fz, j=j, l=l: eng.scalar_tensor_tensor(
                    out=YC[:, j, 0:T], in0=fbuf(fa), scalar=cc(128 + l * 8 + j), in1=fbuf(fz), op0=ALU.mult, op1=ALU.mult),
                    reads=[f"f{fa}", f"f{fz}", "cst"], writes=[f"yc{j}"])
                stA.add(sqi)
            if tidx == 0 and l + 1 < DEPTH:
                sg_prep_pe(l + 1)
                stuff_prep(l + 1)

            stB = Stat(7)
            for g in range(8):
                gg = g % 4
                bu = nb()
                bz = nb()
                bm = nb()
                for si, b in ((0, bu), (1, bz)):
                    n = ubase + 10 + (g // 4) * 2 + si
                    s = n % NSLOT

                    def b_mm(eng, s=s, b=b, gg=gg):
                        wv = WR[:, s, :].rearrange("p (k c) -> p k c", k=8)
                        ins = None
                        for k in range(8):
                            ins = eng.matmul(banks[b][:, 0:T], wv[:, k, gg * 128:(gg + 1) * 128], XN[:, k, 0:T],
                                             start=(k == 0), stop=(k == 7))
                        return ins
                    P.add("pe", b_mm, reads=[f"xn{k}" for k in range(8)] + [f"w{s}"], writes=[f"ps{b}"])
                if gg == 3:
                    for si in range(2):
                        release_unit(ubase + 10 + (g // 4) * 2 + si)

                def m_mm(eng, g=g, bm=bm, l=l):
                    ins = None
                    for tb in range(NTB):
                        ins = eng.matmul(banks[bm][:, tb * 128:(tb + 1) * 128], VN[:, tb, g * 128:(g + 1) * 128],
                                         WMT[:, l, kv, g, :], start=True, stop=True)
                    return ins
                P.add("pe", m_mm, reads=[f"vn{tb}" for tb in range(NTB)] + [f"wmt{l}"], writes=[f"ps{bm}"])
                stA.pairs()
                stA.flush(1 if g == 0 else 0)
                stB.pairs()
                stB.flush(1)
                fz = nf()
                P.add("act", lambda eng, bz=bz, fz=fz: eng.activation(out=fbuf(fz), in_=banks[bz][:, 0:T], func=AF.Silu),
                      reads=[f"ps{bz}"], writes=[f"f{fz}"])
                ft = nf()
                bias = GB[:, 2048 + g * 128:2048 + (g + 1) * 128].unsqueeze(1).broadcast_to([128, NTB, 128])
                if kind == "p":
                    P.add("dve", lambda eng, bm=bm, ft=ft, bias=bias, g=g, l=l: eng.scalar_tensor_tensor(
                        out=fbuf(ft).rearrange("p (a t) -> p a t", a=NTB), in0=banks[bm][:, 0:T].rearrange("p (a t) -> p a t", a=NTB),
                        scalar=cc(200 + l * 8 + g), in1=bias, op0=ALU.mult, op1=ALU.add),
                        reads=[f"ps{bm}", "gbB", "cst"], writes=[f"f{ft}"])
                else:
                    P.add("dve", lambda eng, bm=bm, ft=ft, bias=bias: eng.tensor_tensor(
                        out=fbuf(ft).rearrange("p (a t) -> p a t", a=NTB), in0=banks[bm][:, 0:T].rearrange("p (a t) -> p a t", a=NTB),
                        in1=bias, op=ALU.add), reads=[f"ps{bm}", "gbB"], writes=[f"f{ft}"])
                P.add("dve", lambda eng, bu=bu, ft=ft: eng.tensor_tensor(out=fbuf(ft), in0=banks[bu][:, 0:T], in1=fbuf(ft), op=ALU.mult),
                      reads=[f"ps{bu}", f"f{ft}"], writes=[f"f{ft}"])
                sqi = nsq()
                P.add("act", lambda eng, ft=ft, sqi=sqi: eng.activation(out=SQ[:, sqi, 0:T], in_=fbuf(ft), func=AF.Square),
                      reads=[f"f{ft}"], writes=[f"sq{sqi}"])
                if g == 7:
                    P.add("act", lambda eng: eng.activation(out=JUNK[:, 0:1], in_=ONES[:, 0:1], func=AF.Ln), reads=["ones"])
                P.add("dve", lambda eng, ft=ft, fz=fz, g=g, l=l: eng.scalar_tensor_tensor(
                    out=YC[:, 8 + g, 0:T], in0=fbuf(ft), scalar=cc(160 + l * 8 + g), in1=fbuf(fz), op0=ALU.mult, op1=ALU.mult),
                    reads=[f"f{ft}", f"f{fz}", "cst"], writes=[f"yc{8 + g}"])
                stB.add(sqi)
            assert not stA.raw and not stA.ready and stA.idx == 4
            stB.pairs()
            emit_gbB(tl + 1)
            if l == DEPTH - 1:
                emit_xin_load(tidx + 1)

            stX = Stat(6)
            first = True
            for mp in range(4):
                n = ubase + 14 + mp
                s = n % NSLOT
                ba = [nb(), nb()]
                bbk = [nb(), nb()]
                for part, bks in ((0, ba), (1, bbk)):
                    for mm in range(2):
                        b = bks[mm]

                        def o_mm(eng, s=s, mm=mm, b=b, part=part):
                            wv = WR[:, s, :].rearrange("p (k c) -> p k c", k=16)
                            ins = None
                            for kk in range(8):
                                kc = part * 8 + kk
                                ins = eng.matmul(banks[b][:, 0:T], wv[:, kc, mm * 128:(mm + 1) * 128], YC[:, kc, 0:T],
                                                 start=(kk == 0), stop=(kk == 7))
                            return ins
                        P.add("pe", o_mm, reads=[f"yc{part * 8 + kk}" for kk in range(8)] + [f"w{s}"], writes=[f"ps{b}"])
                    if first and part == 1:
                        stB.flush(0)
                        assert stB.idx == 4 and not stB.raw
                        P.add("act", lambda eng: eng.activation(out=RST[:, 1, 0:T], in_=banks[6][:, 0:T], func=AF.Ln, bias=EPS),
                              reads=["ps6"], writes=["ra"])
                        P.add("act", lambda eng: eng.activation(out=RST[:, 1, 0:T], in_=RST[:, 1, 0:T], func=AF.Exp, scale=-0.5), reads=["ra"], writes=["ra"])
                        P.add("act", lambda eng: eng.activation(out=RST[:, 2, 0:T], in_=banks[7][:, 0:T], func=AF.Ln, bias=EPS),
                              reads=["ps7"], writes=["rb"])
                        P.add("act", lambda eng: eng.activation(out=RST[:, 2, 0:T], in_=RST[:, 2, 0:T], func=AF.Exp, scale=-0.5), reads=["rb"], writes=["rb"])
                        first = False
                release_unit(n)
                f1s = [nf(), nf()]
                f2s = [nf(), nf()]
                for mm in range(2):
                    P.add("dve", lambda eng, b=ba[mm], f1=f1s[mm]: eng.tensor_tensor(out=fbuf(f1), in0=banks[b][:, 0:T], in1=RST[:, 1, 0:T], op=ALU.mult),
                          reads=[f"ps{ba[mm]}", "ra"], writes=[f"f{f1s[mm]}"])
                for mm in range(2):
                    P.add("dve", lambda eng, b=bbk[mm], f2=f2s[mm]: eng.tensor_tensor(out=fbuf(f2), in0=banks[b][:, 0:T], in1=RST[:, 2, 0:T], op=ALU.mult),
                          reads=[f"ps{bbk[mm]}", "rb"], writes=[f"f{f2s[mm]}"])
                for mm in range(2):
                    m = mp * 2 + mm
                    f1 = f1s[mm]
                    f2 = f2s[mm]
                    ae = "dve" if mp == 3 else pe_
                    P.add(ae, lambda eng, f1=f1, f2=f2: eng.tensor_tensor(out=fbuf(f1), in0=fbuf(f1), in1=fbuf(f2), op=ALU.add),
                          reads=[f"f{f1}", f"f{f2}"], writes=[f"f{f1}"])
                    P.add(ae, lambda eng, f1=f1, m=m: eng.tensor_tensor(out=X[:, m, 0:T], in0=X[:, m, 0:T], in1=fbuf(f1), op=ALU.add),
                          reads=[f"f{f1}", f"x{m}"], writes=[f"x{m}"])
                    sqi = nsq()
                    P.add("act", lambda eng, m=m, sqi=sqi: eng.activation(out=SQ[:, sqi, 0:T], in_=X[:, m, 0:T], func=AF.Square),
                          reads=[f"x{m}"], writes=[f"sq{sqi}"])
                    if l + 1 < DEPTH:
                        P.add("act", lambda eng, m=m, l=l: eng.activation(out=XNP[:, m, 0:T], in_=X[:, m, 0:T], func=AF.Identity,
                                                                         scale=cc((l + 1) * 8 + m)),
                              reads=[f"x{m}", "cst"], writes=[f"xnp{m}"])
                    stX.add(sqi)
                stX.flush(0)
                stX.pairs(hold=2)
            if l == DEPTH - 1:
                stX.drain()
            else:
                carry[0] = stX

        P.add("act", lambda eng: eng.activation(out=RST[:, 0, 0:T], in_=banks[6][:, 0:T], func=AF.Ln, bias=EPS),
              reads=["ps6"], writes=["rx"])
        P.add("act", lambda eng: eng.activation(out=RST[:, 0, 0:T], in_=RST[:, 0, 0:T], func=AF.Exp, scale=-0.5), reads=["rx"], writes=["rx"])
        fo = []
        for k in range(8):
            f = nf()
            fo.append(f)
            P.add("dve", lambda eng, k=k, f=f: eng.scalar_tensor_tensor(
                out=FT[:, f, 0:T], in0=X[:, k, 0:T], scalar=cc(192 + k), in1=RST[:, 0, 0:T], op0=ALU.mult, op1=ALU.mult),
                reads=[f"x{k}", "rx", "cst"], writes=[f"f{f}"])
        if tidx + 1 < len(tiles):
            if tiles[tidx + 1][0] == "s":
                sst_prep()
            prologue(tidx + 1)
        for tb in range(NTB):
            for half in range(2):
                b = nb()

                def tr_out(eng, tb=tb, half=half, b=b):
                    ins = None
                    for kk in range(4):
                        k = half * 4 + kk
                        ins = eng.transpose(banks[b][:, kk * 128:(kk + 1) * 128], FT[:, fo[k], tb * 128:(tb + 1) * 128], IDENT)
                    return ins
                P.add("pe", tr_out, reads=[f"f{fo[half * 4 + kk]}" for kk in range(4)] + ["ident"], writes=[f"ps{b}"])
                P.add("act", lambda eng, tb=tb, half=half, b=b: eng.activation(
                    out=XOUT[:, tb, half * 512:(half + 1) * 512], in_=banks[b][:, 0:512], func=AF.Copy),
                    reads=[f"ps{b}"], writes=XOUT_RES[tb])
        ores = []
        for tb in range(NTB):
            ores += XOUT_RES[tb]
        if kind == "p":
            dst = yp[ti * 512:(ti + 1) * 512, :].rearrange("(tb p) d -> p tb d", p=128)
        else:
            dst = ys.rearrange("(tb p) d -> p tb d", p=128)
        P.add("sp", lambda eng: [eng.dma_start(out=dst, in_=XOUT[:, 0:NTB, :])], reads=ores, dma="xoutst")

    def emit_state_out(kind):
        nrow = 2 if kind == "p" else 8
        for l in range(DEPTH):
            for hh in range(2):
                b = nb()

                def st_tr(eng, l=l, b=b, hh=hh):
                    ins = None
                    for jj in range(4):
                        j = hh * 4 + jj
                        src = PST[:, l, j, :] if kind == "p" else SNEW[:, l, j, :]
                        ins = eng.transpose(banks[b][0:nrow, jj * 128:(jj + 1) * 128], src, IDENT)
                    return ins
                rr = ([f"pst{l}_{hh * 4 + jj}" for jj in range(4)] if kind == "p"
                      else [f"snew{l}_{hh * 4 + jj}" for jj in range(4)])
                P.add("pe", st_tr, reads=rr + ["ident"], writes=[f"ps{b}"])
                P.add("act", lambda eng, b=b, hh=hh: eng.activation(
                    out=OSTG[0:nrow, hh * 512:(hh + 1) * 512], in_=banks[b][0:nrow, 0:512], func=AF.Copy),
                    reads=[f"ps{b}"], writes=["ostg"])
            dst = ncp[l] if kind == "p" else ncs[l]
            P.add("sp", lambda eng, dst=dst: [eng.dma_start(out=dst, in_=OSTG[0:nrow, :])], reads=["ostg"], dma="ostst")

    prologue(0)
    for tidx in range(len(tiles)):
        do_tile(tidx)
        kind, ti = tiles[tidx]
        if (kind == "p" and ti == NT - 1) or kind == "s":
            emit_state_out(kind)

    P.finalize()
    with ExitStack() as es:
        sems = {}
        for i, s in enumerate(P.sem_names()):
            sems[s] = es.enter_context(nc.semaphore(f"s{i}"))
        P.emit(nc, sems)
    return nc


_NC_CACHE = {}


def _get_nc(NT=16):
    if NT not in _NC_CACHE:
        _NC_CACHE[NT] = build_nc(NT)
    return _NC_CACHE[NT]


def make_in_maps(inputs, NT=16, n_cores=N_CORES):
    f = lambda a: np.ascontiguousarray(np.asarray(a, dtype=np.float32))
    shared = {k: f(inputs[k]) for k in ("norm_g", "w_in", "conv_w", "sg_ln_g", "sg_ln_b", "sg_w", "sg_b",
                                        "out_g_conv", "out_g_sgu", "w_out", "final_g")}
    xpr = np.asarray(inputs["x_prompt"], dtype=np.float32)
    xsa = np.asarray(inputs["x_sample"], dtype=np.float32)
    stc = np.asarray(inputs["state_conv"], dtype=np.float32)
    maps = []
    for c in range(n_cores):
        m = dict(shared)
        m["xp"] = np.ascontiguousarray(xpr[c, :NT * 512])
        m["xs"] = np.ascontiguousarray(xsa[NSEQ_S * c:NSEQ_S * (c + 1)].reshape(NSEQ_S * LS, D))
        m["sc"] = np.ascontiguousarray(stc[:, NSEQ_S * c:NSEQ_S * (c + 1)])
        maps.append(m)
    return maps


def kernel(x_prompt, x_sample, state_conv, norm_g, w_in, conv_w, sg_ln_g, sg_ln_b,
           sg_w, sg_b, out_g_conv, out_g_sgu, w_out, final_g):
    inputs = dict(x_prompt=x_prompt, x_sample=x_sample, state_conv=state_conv, norm_g=norm_g, w_in=w_in,
                  conv_w=conv_w, sg_ln_g=sg_ln_g, sg_ln_b=sg_ln_b, sg_w=sg_w, sg_b=sg_b,
                  out_g_conv=out_g_conv, out_g_sgu=out_g_sgu, w_out=w_out, final_g=final_g)
    nc = _get_nc(16)
    in_maps = make_in_maps(inputs, 16)
    res = run_bass_kernel_spmd(nc, in_maps, core_ids=list(range(N_CORES)))
    B = N_CORES
    y_prompt = np.empty((B, SEQ, D), np.float32)
    y_sample = np.empty((B * NSEQ_S, LS, D), np.float32)
    ncp_o = np.empty((DEPTH, B, 2, D), np.float32)
    ncs_o = np.empty((DEPTH, B * NSEQ_S, 2, D), np.float32)
    nv_o = np.empty((DEPTH, B * NSEQ_S, LS, D), np.float32)
    for c, r in enumerate(res.results):
        y_prompt[c] = r["yp"]
        y_sample[NSEQ_S * c:NSEQ_S * (c + 1)] = r["ys"].reshape(NSEQ_S, LS, D)
        ncp_o[:, c] = r["ncp"]
        ncs_o[:, NSEQ_S * c:NSEQ_S * (c + 1)] = r["ncs"].reshape(DEPTH, NSEQ_S, 2, D)
        nv_o[:, NSEQ_S * c:NSEQ_S * (c + 1)] = r["nv"].reshape(DEPTH, NSEQ_S, LS, D)
    return (y_prompt, y_sample, ncp_o, ncs_o, nv_o)
```

```python
import numpy as np
from contextlib import ExitStack
import concourse.bass as bass
import concourse.mybir as mybir
from concourse.bass_utils import run_bass_kernel_spmd

F32 = mybir.dt.float32
BF16 = mybir.dt.bfloat16
AF = mybir.ActivationFunctionType
ALU = mybir.AluOpType

D = 1024
DEPTH = 4
NCOLS = 7168
EPS = 1e-6
NSLOT = 8
NUNIT = 18
N_CORES = 8
SEQ = 8192
NSEQ_S = 4
LS = 64


class Prog:
    ENGS = ("pe", "act", "dve", "pool", "sp")

    def __init__(self):
        self.ops = []
        self.lw = {}
        self.rd = {}
        self.eng_ops = {e: [] for e in self.ENGS}
        self.dma_cnt = {}
        self.debug = None

    def add(self, eng, fn, reads=(), writes=(), dma=None, ndma=1):
        i = len(self.ops)
        deps = set()
        for r in reads:
            if r in self.lw:
                deps.add(self.lw[r])
        for w in writes:
            if w in self.lw:
                deps.add(self.lw[w])
            for q in self.rd.get(w, ()):
                deps.add(q)
        deps.discard(i)
        for r in reads:
            self.rd.setdefault(r, []).append(i)
        for w in writes:
            self.lw[w] = i
            self.rd[w] = []
        op = dict(id=i, eng=eng, fn=fn, deps=deps, dma=dma, sig=None, need=False,
                  pos=len(self.eng_ops[eng]),
                  tag=",".join(list(writes)[:2]) + "<-" + ",".join(list(reads)[:3]))
        if dma is not None:
            dma = dma + ("@sw" if eng == "pool" else "@hw")
            op["dma"] = dma
            self.dma_cnt[dma] = self.dma_cnt.get(dma, 0) + ndma
            op["sig"] = (dma, 16 * self.dma_cnt[dma])
        self.ops.append(op)
        self.eng_ops[eng].append(op)
        return i

    def finalize(self):
        for op in self.ops:
            waits = []
            for d in sorted(op["deps"]):
                p = self.ops[d]
                if p["dma"] is None and p["eng"] == op["eng"]:
                    if op["eng"] == "pe":
                        continue
                    if op["pos"] - p["pos"] > 2:
                        continue
                waits.append(d)
                if p["dma"] is None:
                    p["need"] = True
            op["waits"] = waits
        for e, lst in self.eng_ops.items():
            c = 0
            for op in lst:
                if op["dma"] is None and op["need"]:
                    c += 1
                    op["sig"] = (("eng", e), c)

    def sem_names(self):
        return [("eng", e) for e in self.ENGS] + list(self.dma_cnt.keys())

    def emit(self, nc, sems):
        def run(e):
            def body(eng):
                waited = {}
                for op in self.eng_ops[e]:
                    for d in op["waits"]:
                        s, v = self.ops[d]["sig"]
                        if waited.get(s, 0) >= v:
                            continue
                        w = eng.wait_ge(sems[s], v)
                        waited[s] = v
                        if self.debug is not None:
                            self.debug["wait"][w.ins.name] = (op["id"], d)
                    r = op["fn"](eng)
                    if self.debug is not None:
                        last = r[-1] if isinstance(r, list) else r
                        self.debug["inst"][last.ins.name] = op["id"]
                    if op["dma"] is not None:
                        for ins in r:
                            ins.then_inc(sems[op["dma"]], 16)
                    elif op["need"]:
                        r.then_inc(sems[("eng", e)], 1)
                if e == "sp":
                    for s, c in self.dma_cnt.items():
                        eng.wait_ge(sems[s], 16 * c)
            return body

        with nc.Block() as block:
            block.tensor(run("pe"))
            block.scalar(run("act"))
            block.vector(run("dve"))
            block.gpsimd(run("pool"))
            block.sync(run("sp"))


def build_nc(NT=16, debug=None):
    nc = bass.Bass("TRN2", target_bir_lowering=False)
    SEQP = NT * 512

    def din(name, shape):
        return nc.dram_tensor(name, list(shape), F32, kind="ExternalInput").ap()

    def dout(name, shape):
        return nc.dram_tensor(name, list(shape), F32, kind="ExternalOutput").ap()

    xp = din("xp", [SEQP, D])
    xs = din("xs", [NSEQ_S * LS, D])
    sc = din("sc", [DEPTH, NSEQ_S, 2, D])
    ngd = din("norm_g", [DEPTH, D])
    win = din("w_in", [DEPTH, D, NCOLS])
    cwd = din("conv_w", [DEPTH, 3, D])
    lngd = din("sg_ln_g", [DEPTH, D])
    lnbd = din("sg_ln_b", [DEPTH, D])
    sgwd = din("sg_w", [DEPTH, 8, 128, 128])
    sgbd = din("sg_b", [DEPTH, 8, 128])
    ogad = din("out_g_conv", [DEPTH, D])
    ogbd = din("out_g_sgu", [DEPTH, D])
    woutd = din("w_out", [DEPTH, 2048, D])
    fgd = din("final_g", [D])

    yp = dout("yp", [SEQP, D])
    ys = dout("ys", [NSEQ_S * LS, D])
    ncp = dout("ncp", [DEPTH, 2, D])
    ncs = dout("ncs", [DEPTH, NSEQ_S * 2, D])
    nvo = dout("nv", [DEPTH, NSEQ_S * LS, D])

    wsc = nc.dram_tensor("wsc", [DEPTH, NUNIT, 128, 4096], BF16, kind="Internal").ap()
    stf = nc.dram_tensor("stf", [DEPTH, 128, 1024], F32, kind="Internal").ap()

    base0 = (nc.sbuf_base + 63) // 64 * 64
    cur = [base0]

    def sb(name, shape, dtype, at=None):
        nbytes = int(np.prod(shape[1:])) * (2 if dtype == BF16 else 4)
        if at is None:
            off = cur[0]
            cur[0] += (nbytes + 63) // 64 * 64
        else:
            off = at
        t = nc.alloc_sbuf_tensor_at(name, list(shape), dtype, offset=off)
        return t, off

    NF = 12
    ident_t, _ = sb("ident", [128, 128], F32)
    ones_t, _ = sb("ones", [128, 128], BF16)
    mh_t, _ = sb("mh", [128, 8], F32)
    cst_t, _ = sb("cst", [128, 232], F32)
    stg_t, _ = sb("stg", [128, 2, 128], F32)
    pst_t, _ = sb("pst", [128, DEPTH, 8, 2], F32)
    sst_t, _ = sb("sst", [128, DEPTH * 8 * 8], F32)
    snew_t, _ = sb("snew", [128, DEPTH, 8, 8], F32)
    ostg_t, _ = sb("ostg", [128, 1024], F32)
    x_t, x_off = sb("xres", [128, 8, 512], F32)
    xn_t, xn_off = sb("xn", [128, 8, 512], BF16)
    vn_t, vn_off = sb("vn", [128, 4, 1024], BF16)
    yc_t, yc_off = sb("ycat", [128, 16, 512], BF16)
    sq_t, _ = sb("sq", [128, 6, 512], BF16)
    ft_t, _ = sb("ft", [128, NF, 516], F32)
    rst_t, _ = sb("rst", [128, 3, 512], F32)
    wr_t, _ = sb("wring", [128, NSLOT, 4096], BF16)
    wmt_t, _ = sb("wmt", [128, DEPTH, 2, 8, 128], BF16)
    gb_t, _ = sb("gb", [128, 3072], F32)
    sgs_t, _ = sb("sgst", [128, 2, 8, 128], F32)
    bnst_t, _ = sb("bnst", [128, 4, 2, 6], F32)
    mv_t, _ = sb("mv", [128, 4, 4], F32)
    vstg_t, _ = sb("vstg", [128, 2, 512], F32)
    junk_t, _ = sb("junk", [128, 512], BF16)
    xnp_t, _ = sb("xnp", [128, 8, 512], BF16)
    dg_t, _ = sb("dg", [128, 4, 128], BF16)
    epv_t, _ = sb("epv", [128, 4], F32)
    assert xn_off + 8192 == vn_off
    xin_t, _ = sb("xin", [128, 4, 1024], F32, at=xn_off)
    xout_t, _ = sb("xout", [128, 4, 1024], F32, at=yc_off)
    sstg_t, _ = sb("sstg", [128, DEPTH, 1024], F32, at=yc_off)
    assert cur[0] <= nc.sbuf_top, (cur[0], nc.sbuf_top)
    nc.alloc_sbuf_tensor("arena", [128, (cur[0] - nc.sbuf_base + 3) // 4], F32)

    IDENT = ident_t[:]
    ONES = ones_t[:]
    MH = mh_t[:]
    CST = cst_t[:]
    STG = stg_t[:]
    PST = pst_t[:]
    SST = sst_t[:]
    SNEW = snew_t[:]
    OSTG = ostg_t[:]
    X = x_t[:]
    XN = xn_t[:]
    VN = vn_t[:]
    YC = yc_t[:]
    SQ = sq_t[:]
    FT = ft_t[:]
    RST = rst_t[:]
    WR = wr_t[:]
    WMT = wmt_t[:]
    GB = gb_t[:]
    BNST = bnst_t[:]
    MV = mv_t[:]
    VSTG = vstg_t[:]
    JUNK = junk_t[:]
    XNP = xnp_t[:]
    DG = dg_t[:]
    EPV = epv_t[:]
    XIN = xin_t[:]
    XOUT = xout_t[:]
    SG = sgs_t[:]
    SSTG = sstg_t[:]

    banks = [nc.alloc_psum_tensor(f"bank{b}", [128, 512], F32)[:] for b in range(8)]

    P = Prog()
    if debug is not None:
        debug["wait"] = {}
        debug["inst"] = {}
        debug["P"] = P
        P.debug = debug

    XALL = [f"x{k}" for k in range(8)]
    YCALL = [f"yc{i}" for i in range(16)]
    XIN_RES = [[f"xn{k}" for k in range(0, 4)], [f"xn{k}" for k in range(4, 8)],
               ["vn0", "vn1"], ["vn2", "vn3"]]
    XOUT_RES = [[f"yc{i}" for i in range(4 * tb, 4 * tb + 4)] for tb in range(4)]

    rot = {"ps": 0, "f": 0, "sq": 0, "vst": 0}

    def nb():
        b = rot["ps"]
        rot["ps"] = (b + 1) % 6
        return b

    def nf():
        i = rot["f"]
        rot["f"] = (i + 1) % NF
        return i

    def nsq():
        i = rot["sq"]
        rot["sq"] = (i + 1) % 6
        return i

    def cc(col):
        return CST[:, col:col + 1]

    def p0_ident(eng):
        return eng.affine_select(out=IDENT, in_=IDENT, pattern=[[-1, 128]], compare_op=ALU.not_equal,
                                 fill=1.0, base=0, channel_multiplier=1)

    UNIT_COL = [5120, 5632, 0, 1024, 2048, 3072, 512, 1536, 2560, 3584, 4096, 6144, 4608, 6656]

    def cast_unit(eng, l, u):
        if u < 14:
            c0 = UNIT_COL[u]
            return eng.dma_start(out=wsc[l, u].rearrange("p (k c) -> p k c", k=8),
                                 in_=win[l, :, c0:c0 + 512].rearrange("(k p) c -> p k c", p=128))
        mp = u - 14
        return eng.dma_start(out=wsc[l, u].rearrange("p (k c) -> p k c", k=16),
                             in_=woutd[l, :, mp * 256:(mp + 1) * 256].rearrange("(k p) c -> p k c", p=128))

    cast_pending = []
    cast_ops = {}

    def emit_cast0(units, l=0):
        for u in units:
            P.add("pool", lambda eng, u=u, l=l: [cast_unit(eng, l, u)], writes=[f"wsc{l}_{u}"], dma=f"c{l}_{u}")

    CAST_GROUP = 5

    def emit_cast(l):
        groups = {}
        for u in range(NUNIT):
            grp = u // CAST_GROUP
            i = P.add("pool", lambda eng, l=l, u=u: [cast_unit(eng, l, u)], writes=[f"wsc{l}_{u}"], dma=f"cast{l}g{grp}")
            groups.setdefault(grp, []).append(i)
        for grp, lst in groups.items():
            for j in lst:
                P.ops[j]["sig"] = (P.ops[j]["dma"], 16 * len(lst))
        cast_ops[l] = True

    def p0_const_load(eng):
        r = []
        r.append(eng.dma_start(out=STG[0:32, 0, :], in_=ngd.rearrange("l (k c) -> (l k) c", c=128)))
        r.append(eng.dma_start(out=STG[32:128, 0, :], in_=cwd.rearrange("l r (k c) -> (l r k) c", c=128)))
        r.append(eng.dma_start(out=STG[0:32, 1, :], in_=ogad.rearrange("l (k c) -> (l k) c", c=128)))
        r.append(eng.dma_start(out=STG[32:64, 1, :], in_=ogbd.rearrange("l (k c) -> (l k) c", c=128)))
        r.append(eng.dma_start(out=STG[64:72, 1, :], in_=fgd.rearrange("(k c) -> k c", c=128)))
        r.append(eng.dma_start(out=STG[72:104, 1, :], in_=lngd.rearrange("l (k c) -> (l k) c", c=128)))
        return r

    def p0_const_tr(eng):
        eng.transpose(banks[0][:, 0:128], STG[:, 0, :], IDENT)
        return eng.transpose(banks[0][:, 128:232], STG[0:104, 1, :], IDENT[0:104, 0:104])

    def sst_prep():
        P.add("sp", lambda eng: [eng.dma_start(out=SSTG[0:8, :, :], in_=sc.rearrange("l s r c -> (s r) l c"))],
              writes=YCALL, dma="scload")
        b = nb()

        def p0_sst_tr(eng):
            ins = None
            for l in range(DEPTH):
                for j in range(8):
                    c0 = (l * 8 + j) * 8
                    ins = eng.transpose(banks[b][:, c0:c0 + 8], SSTG[0:8, l, j * 128:(j + 1) * 128], IDENT[0:8, 0:8])
            return ins
        P.add("pe", p0_sst_tr, reads=YCALL + ["ident"], writes=[f"ps{b}"])
        P.add("act", lambda eng: eng.activation(out=SST, in_=banks[b][:, 0:256], func=AF.Copy),
              reads=[f"ps{b}"], writes=["sst"])

    ldq = ["act"]

    def sg_prep_load(l, first_tile=False):
        P.add("dve", lambda eng: eng.memset(SG[:, 1, :, :], 0.0), writes=["sg"])

        def sg_load(eng, l=l):
            r = []
            r.append(eng.dma_start(out=SG[:, 0, :, :], in_=sgwd[l].rearrange("g t s -> t g s")))
            r.append(eng.dma_start(out=SG[0:64, 1, :, 0:64], in_=sgwd[l, :, 0:64, 0:64].rearrange("g t s -> t g s")))
            r.append(eng.dma_start(out=SG[64:128, 1, :, 64:128], in_=sgwd[l, :, 0:64, 0:64].rearrange("g t s -> t g s")))
            return r
        P.add(ldq[0], sg_load, writes=["sg"], dma="sgload", ndma=3)

    def sg_prep_pe(l):
        for v in range(2):
            for gh in range(2):
                b = nb()

                def sg_tr(eng, v=v, gh=gh, b=b):
                    ins = None
                    for gi in range(4):
                        ins = eng.transpose(banks[b][:, gi * 128:(gi + 1) * 128], SG[:, v, gh * 4 + gi, :], IDENT)
                    return ins
                P.add("pe", sg_tr, reads=["sg", "ident"], writes=[f"ps{b}"])
                P.add("act", lambda eng, l=l, v=v, gh=gh, b=b: eng.activation(
                    out=WMT[:, l, v, gh * 4:(gh + 1) * 4, :],
                    in_=banks[b][:, 0:512].rearrange("p (g t) -> p g t", g=4), func=AF.Copy),
                    reads=[f"ps{b}"], writes=[f"wmt{l}"])
        P.add("dve", lambda eng, l=l: eng.memset(WMT[64:128, l, 0, :, 0:64], 0.0),
              reads=[f"wmt{l}"], writes=[f"wmt{l}"])

    def stuff_prep(l):
        fb = [nf(), nf()]
        fc2 = [nf(), nf()]
        sqs = [nsq(), nsq()]

        def ld(eng):
            r = []
            for h in range(2):
                r.append(eng.dma_start(out=FT[:, fb[h], 0:512], in_=lnbd[l:l + 1, h * 512:(h + 1) * 512].broadcast_to([128, 512])))
                r.append(eng.dma_start(out=FT[:, fc2[h], 0:512].rearrange("p (g t) -> p g t", g=4),
                                       in_=sgbd[l:l + 1, h * 4:(h + 1) * 4, :].broadcast_to([128, 4, 128])))
            return r
        P.add(ldq[0], ld, writes=[f"f{x}" for x in fb + fc2], dma=f"stfl{l}", ndma=4)
        for h in range(2):
            P.add("act", lambda eng, h=h: eng.activation(out=SQ[:, sqs[h], 0:512], in_=FT[:, fb[h], 0:512], func=AF.Copy),
                  reads=[f"f{fb[h]}"], writes=[f"sq{sqs[h]}"])
            b = nb()

            def st_mm(eng, h=h, b=b):
                ins = None
                for gi in range(4):
                    ins = eng.matmul(banks[b][:, gi * 128:(gi + 1) * 128], SQ[:, sqs[h], gi * 128:(gi + 1) * 128],
                                     WMT[:, l, 0, h * 4 + gi, :], start=True, stop=True)
                return ins
            P.add("pe", st_mm, reads=[f"sq{sqs[h]}", f"wmt{l}"], writes=[f"ps{b}"])
            P.add("dve", lambda eng, h=h, b=b: eng.tensor_tensor(out=FT[:, fc2[h], 0:512], in0=banks[b][:, 0:512],
                                                                 in1=FT[:, fc2[h], 0:512], op=ALU.add),
                  reads=[f"ps{b}", f"f{fc2[h]}"], writes=[f"f{fc2[h]}"])
        P.add("sp", lambda eng: [eng.dma_start(out=stf[l, :, h * 512:(h + 1) * 512], in_=FT[:, fc2[h], 0:512]) for h in range(2)],
              reads=[f"f{x}" for x in fc2], writes=[f"stf{l}"], dma=f"stfs{l}", ndma=2)

    tiles = [("p", ti) for ti in range(NT)] + [("s", 0)]
    NTL = len(tiles) * DEPTH
    total_units = NTL * NUNIT
    loaded = [0]

    def emit_wload(n):
        l = (n // NUNIT) % DEPTH
        u = n % NUNIT
        s = n % NSLOT
        P.add("sp", lambda eng, l=l, u=u, s=s: [eng.dma_start(out=WR[:, s, :], in_=wsc[l, u])],
              reads=[f"wsc{l}_{u}"], writes=[f"w{s}"], dma=f"wl{s}")
        assert l <= 1 or cast_ops.get(l)

    def release_unit(n):
        nn = n + NSLOT
        if nn < total_units:
            emit_wload(nn)

    def emit_gbA(tl):
        if tl >= NTL or tiles[tl // DEPTH][0] == "p":
            return
        l = tl % DEPTH
        P.add(ldq[0], lambda eng: [
            eng.dma_start(out=GB[:, 0:1024], in_=lngd[l:l + 1, :].broadcast_to([128, 1024])),
            eng.dma_start(out=GB[:, 1024:2048], in_=lnbd[l:l + 1, :].broadcast_to([128, 1024]))],
            writes=["gbA"], dma="gblA", ndma=2)

    def emit_gbB(tl):
        if tl >= NTL:
            return
        kind = tiles[tl // DEPTH][0]
        l = tl % DEPTH

        def fn(eng):
            r = []
            if kind == "p":
                r.append(eng.dma_start(out=GB[:, 2048:3072], in_=stf[l]))
            else:
                gv = GB[:, 2048:3072].rearrange("p (g r t) -> p g r t", g=8, r=2)
                for rr in range(2):
                    r.append(eng.dma_start(out=gv[:, :, rr, :], in_=sgbd[l:l + 1, :, 0:64].broadcast_to([128, 8, 64])))
            return r
        P.add(ldq[0], fn, reads=([f"stf{l}"] if kind == "p" else []), writes=["gbB"], dma="gblB", ndma=(1 if kind == "p" else 2))

    def emit_xin_load(tidx):
        if tidx >= len(tiles):
            return
        kind, ti = tiles[tidx]
        if kind == "p":
            src = xp[ti * 512:(ti + 1) * 512, :].rearrange("(tb p) d -> p tb d", p=128)
            ntb = 4
        else:
            src = xs.rearrange("(tb p) d -> p tb d", p=128)
            ntb = 2
        res = []
        for tb in range(ntb):
            res += XIN_RES[tb]
        P.add(ldq[0], lambda eng: [eng.dma_start(out=XIN[:, 0:ntb, :], in_=src)], writes=res, dma="xinl")

    emit_xin_load(0)
    emit_cast0([0, 1])
    P.add("pool", lambda eng: eng.memset(IDENT, 0.0), writes=["ident"])
    P.add("pool", p0_ident, reads=["ident"], writes=["ident"])
    P.add("pool", lambda eng: eng.memset(ONES, 1.0 / 1024.0), writes=["ones"])
    P.add("pool", lambda eng: eng.memset(MH, -0.5), writes=["mh"])
    P.add("pool", lambda eng: eng.memset(PST, 0.0), writes=[f"pst{l}_{j}" for l in range(DEPTH) for j in range(8)])
    P.add("act", p0_const_load, writes=["stg"], dma="cload", ndma=6)
    sg_prep_load(0)
    emit_cast0(range(2, NUNIT))
    emit_cast0(range(NUNIT), l=1)
    for l in range(2, DEPTH):
        emit_cast(l)
    P.add("pe", p0_const_tr, reads=["stg", "ident"], writes=["ps0"])
    P.add("act", lambda eng: eng.activation(out=CST, in_=banks[0][:, 0:232], func=AF.Copy),
          reads=["ps0"], writes=["cst"])
    rot["ps"] = 1
    for n in range(min(NSLOT, total_units)):
        emit_wload(n)
    sg_prep_pe(0)
    stuff_prep(0)
    emit_gbA(0)
    emit_gbB(0)

    def prologue(tidx):
        kind, ti = tiles[tidx]
        T = 512 if kind == "p" else 256
        NTB = T // 128
        pend = []

        def flush_stats(bankno, n_total, keep):
            while len(pend) > keep:
                sqi, idx = pend.pop(0)
                P.add("pe", lambda eng, sqi=sqi, idx=idx: eng.matmul(
                    banks[bankno][:, 0:T], ONES, SQ[:, sqi, 0:T], start=(idx == 0), stop=(idx == n_total - 1)),
                    reads=[f"sq{sqi}", "ones"], writes=[f"ps{bankno}"])

        for k in range(8):
            b = nb()

            def tr_in(eng, k=k, b=b):
                ins = None
                for tb in range(NTB):
                    ins = eng.transpose(banks[b][:, tb * 128:(tb + 1) * 128], XIN[:, tb, k * 128:(k + 1) * 128], IDENT)
                return ins
            rres = []
            for tb in range(NTB):
                rres += XIN_RES[tb]
            P.add("pe", tr_in, reads=rres + ["ident"], writes=[f"ps{b}"])
            P.add("act", lambda eng, k=k, b=b: eng.activation(out=XNP[:, k, 0:T], in_=banks[b][:, 0:T], func=AF.Identity, scale=cc(k)),
                  reads=[f"ps{b}", "cst"], writes=[f"xnp{k}"])
            sqi = nsq()
            P.add("act", lambda eng, b=b, sqi=sqi: eng.activation(out=SQ[:, sqi, 0:T], in_=banks[b][:, 0:T], func=AF.Square),
                  reads=[f"ps{b}"], writes=[f"sq{sqi}"])
            P.add("act", lambda eng, k=k, b=b: eng.activation(out=X[:, k, 0:T], in_=banks[b][:, 0:T], func=AF.Copy),
                  reads=[f"ps{b}"], writes=[f"x{k}"])
            pend.append((sqi, k))
            flush_stats(6, 8, 1)
        flush_stats(6, 8, 0)

    def do_tile(tidx):
        kind, ti = tiles[tidx]
        T = 512 if kind == "p" else 256
        NTB = T // 128
        nseg = 1 if kind == "p" else NSEQ_S
        L = T // nseg
        kv = 0 if kind == "p" else 1
        pe_ = "dve" if tidx == 0 else "pool"
        ldq[0] = "act" if tidx == 0 else "pool"

        def seg(ap):
            return ap.rearrange("p (s c) -> p s c", s=nseg)

        def fbuf(i):
            return FT[:, i, 0:T]

        class Stat:
            def __init__(self, bank):
                self.bank, self.raw, self.ready, self.idx = bank, [], [], 0

            def add(self, sqi):
                self.raw.append(sqi)

            def pairs(self, hold=0):
                while len(self.raw) - hold >= 2:
                    a, b2 = self.raw.pop(0), self.raw.pop(0)
                    sn = nsq()
                    P.add("dve", lambda eng, a=a, b2=b2, sn=sn: eng.tensor_tensor(out=SQ[:, sn, 0:T], in0=SQ[:, a, 0:T], in1=SQ[:, b2, 0:T], op=ALU.add),
                          reads=[f"sq{a}", f"sq{b2}"], writes=[f"sq{sn}"])
                    self.ready.append((sn, self.idx))
                    self.idx += 1

            def flush(self, keep=0):
                while len(self.ready) > keep:
                    sqi, idx = self.ready.pop(0)
                    bankno = self.bank
                    P.add("pe", lambda eng, sqi=sqi, idx=idx, bankno=bankno: eng.matmul(
                        banks[bankno][:, 0:T], ONES, SQ[:, sqi, 0:T], start=(idx == 0), stop=(idx == 3)),
                        reads=[f"sq{sqi}", "ones"], writes=[f"ps{bankno}"])

            def drain(self):
                self.pairs()
                self.flush(0)
                assert self.idx == 4 and not self.raw

        carry = [None]
        for l in range(DEPTH):
            tl = tidx * DEPTH + l
            par = tl % 2
            ubase = tl * NUNIT

            def emit_rx():
                if carry[0] is not None:
                    carry[0].drain()
                    carry[0] = None
                P.add("act", lambda eng: eng.activation(out=RST[:, 0, 0:T], in_=banks[6][:, 0:T], func=AF.Ln, bias=EPS),
                      reads=["ps6"], writes=["rx"])
                P.add("act", lambda eng: eng.activation(out=RST[:, 0, 0:T], in_=RST[:, 0, 0:T], func=AF.Exp, scale=-0.5),
                      reads=["rx"], writes=["rx"])

            def emit_epv():
                for tb in range(NTB):
                    P.add("dve", lambda eng, tb=tb: eng.tensor_tensor(out=DG[:, tb, :], in0=RST[:, 0, tb * 128:(tb + 1) * 128], in1=IDENT, op=ALU.mult),
                          reads=["rx", "ident"], writes=[f"dg{tb}"])

            def emit_epv2():
                be = nb()

                def ep_mm(eng, be=be):
                    ins = None
                    for tb in range(NTB):
                        ins = eng.matmul(banks[be][:, tb:tb + 1], DG[:, tb, :], ONES[:, 0:1], start=True, stop=True)
                    return ins
                P.add("pe", ep_mm, reads=[f"dg{tb}" for tb in range(NTB)] + ["ones"], writes=[f"ps{be}"])
                fe = nf()
                P.add("act", lambda eng, be=be, fe=fe: eng.activation(out=FT[:, fe, 0:NTB], in_=banks[be][:, 0:NTB], func=AF.Copy),
                      reads=[f"ps{be}"], writes=[f"f{fe}"])
                P.add("dve", lambda eng, fe=fe: eng.scalar_tensor_tensor(
                    out=EPV[:, 0:NTB], in0=FT[:, fe, 0:NTB], scalar=1024.0 * 1024.0 / EPS, in1=FT[:, fe, 0:NTB],
                    op0=ALU.mult, op1=ALU.mult), reads=[f"f{fe}"], writes=["epv"])
                P.add("dve", lambda eng: eng.reciprocal(EPV[:, 0:NTB], EPV[:, 0:NTB]), reads=["epv"], writes=["epv"])

            def emit_xn():
                for k in range(8):
                    P.add("dve", lambda eng, k=k, l=l: eng.scalar_tensor_tensor(
                        out=XN[:, k, 0:T], in0=X[:, k, 0:T], scalar=cc(l * 8 + k), in1=RST[:, 0, 0:T],
                        op0=ALU.mult, op1=ALU.mult), reads=[f"x{k}", "rx", "cst"], writes=[f"xn{k}"])


            if tidx == 0 and l + 1 < DEPTH:
                sg_prep_load(l + 1, first_tile=True)

            vfis = {}

            def v_post2A(tbp, vbank):
                fis = {}
                sres = []
                for tb in (tbp, tbp + 1):
                    for half in range(2):
                        b = vbank[(tb, half)]
                        fi = nf()
                        fis[(tb, half)] = fi
                        P.add("act", lambda eng, b=b, fi=fi, tb=tb, half=half: eng.activation(
                            out=FT[:, fi, 0:512], in_=banks[b][:, 0:512], func=AF.Copy, accum_out=BNST[:, tb, half, 0:1]),
                            reads=[f"ps{b}"], writes=[f"f{fi}", f"bnst{tb}_{half}a"])
                for tb in (tbp, tbp + 1):
                    for half in range(2):
                        fi = fis[(tb, half)]
                        P.add("act", lambda eng, fi=fi, tb=tb, half=half: eng.activation(
                            out=JUNK, in_=FT[:, fi, 0:512], func=AF.Square, accum_out=BNST[:, tb, half, 1:2]),
                            reads=[f"f{fi}"], writes=[f"bnst{tb}_{half}b"])
                        sres += [f"bnst{tb}_{half}a", f"bnst{tb}_{half}b"]
                vfis[tbp] = (fis, sres)

            def v_post2B(tbp):
                fis, sres = vfis[tbp]
                mvr = f"mv{tbp}"
                M2 = MV[:, tbp:tbp + 2, :]
                P.add("dve", lambda eng: eng.tensor_tensor(out=M2[:, :, 0:2], in0=BNST[:, tbp:tbp + 2, 0, 0:2],
                                                           in1=BNST[:, tbp:tbp + 2, 1, 0:2], op=ALU.add),
                      reads=sres, writes=[mvr])
                P.add("dve", lambda eng: eng.scalar_tensor_tensor(
                    out=M2[:, :, 2:3], in0=M2[:, :, 0:1], scalar=1.0 / (1024.0 * 1024.0), in1=M2[:, :, 0:1], op0=ALU.mult, op1=ALU.mult),
                    reads=[mvr], writes=[mvr])
                P.add("dve", lambda eng: eng.scalar_tensor_tensor(
                    out=M2[:, :, 1:2], in0=M2[:, :, 1:2], scalar=1.0 / 1024.0, in1=M2[:, :, 2:3], op0=ALU.mult, op1=ALU.subtract),
                    reads=[mvr], writes=[mvr])
                P.add("dve", lambda eng: eng.scalar_tensor_tensor(out=M2[:, :, 2:3], in0=M2[:, :, 1:2], scalar=0.0,
                                                                  in1=EPV[:, tbp:tbp + 2].unsqueeze(2), op0=ALU.max, op1=ALU.add),
                      reads=[mvr, "epv"], writes=[mvr])
                P.add("act", lambda eng: eng.activation(out=M2[:, :, 2:3], in_=M2[:, :, 2:3], func=AF.Ln),
                      reads=[mvr], writes=[mvr])
                P.add("act", lambda eng: eng.activation(out=M2[:, :, 2:3], in_=M2[:, :, 2:3], func=AF.Exp, scale=-0.5),
                      reads=[mvr], writes=[mvr])
                P.add("dve", lambda eng: eng.scalar_tensor_tensor(
                    out=M2[:, :, 3:4], in0=M2[:, :, 0:1], scalar=-1.0 / 1024.0, in1=M2[:, :, 2:3], op0=ALU.mult, op1=ALU.mult),
                    reads=[mvr], writes=[mvr])
                for tb in (tbp, tbp + 1):
                    for half in range(2):
                        fi = fis[(tb, half)]
                        if kind == "p":
                            P.add(pe_, lambda eng, tb=tb, fi=fi, half=half: eng.tensor_scalar(
                                out=VN[:, tb, half * 512:(half + 1) * 512], in0=FT[:, fi, 0:512], scalar1=MV[:, tb, 2:3],
                                scalar2=MV[:, tb, 3:4], op0=ALU.mult, op1=ALU.add),
                                reads=[f"f{fi}", mvr], writes=[f"vn{tb}"])
                            continue
                        P.add(pe_, lambda eng, tb=tb, fi=fi: eng.tensor_scalar(
                            out=FT[:, fi, 0:512], in0=FT[:, fi, 0:512], scalar1=MV[:, tb, 2:3], scalar2=MV[:, tb, 3:4],
                            op0=ALU.mult, op1=ALU.add),
                            reads=[f"f{fi}", mvr], writes=[f"f{fi}"])
                        P.add(pe_, lambda eng, fi=fi, half=half, par=par: eng.tensor_tensor(
                            out=FT[:, fi, 0:512], in0=FT[:, fi, 0:512], in1=GB[:, half * 512:(half + 1) * 512], op=ALU.mult),
                            reads=[f"f{fi}", "gbA"], writes=[f"f{fi}"])
                        if kind == "p":
                            P.add(pe_, lambda eng, fi=fi, half=half, par=par, tb=tb: eng.tensor_tensor(
                                out=VN[:, tb, half * 512:(half + 1) * 512], in0=FT[:, fi, 0:512],
                                in1=GB[:, 1024 + half * 512:1024 + (half + 1) * 512], op=ALU.add),
                                reads=[f"f{fi}", "gbA"], writes=[f"vn{tb}"])
                        else:
                            vi = rot["vst"]
                            rot["vst"] = 1 - vi
                            P.add(pe_, lambda eng, fi=fi, half=half, par=par, vi=vi: eng.tensor_tensor(
                                out=VSTG[:, vi, :], in0=FT[:, fi, 0:512],
                                in1=GB[:, 1024 + half * 512:1024 + (half + 1) * 512], op=ALU.add),
                                reads=[f"f{fi}", "gbA"], writes=[f"vst{vi}"])
                            P.add("act", lambda eng, vi=vi, tb=tb, half=half: eng.activation(
                                out=VN[:, tb, half * 512:(half + 1) * 512], in_=VSTG[:, vi, :], func=AF.Copy),
                                reads=[f"vst{vi}"], writes=[f"vn{tb}"])
                            P.add("sp", lambda eng, vi=vi, tb=tb, half=half, l=l: [eng.dma_start(
                                out=nvo[l, tb * 128:(tb + 1) * 128, half * 512:(half + 1) * 512], in_=VSTG[:, vi, :])],
                                reads=[f"vst{vi}"], dma=f"vsto{vi}")

            vbank = {}
            for tbp in range(0, NTB, 2):
                bm4 = {(tb, half): nb() for tb in (tbp, tbp + 1) for half in range(2)}
                vbank.update(bm4)
                sl = [(ubase + half) % NSLOT for half in range(2)]
                if tbp == 0:
                    for k in range(8):
                        def v_mm(eng, k=k, bm4=bm4, sl=sl):
                            ins = None
                            for (tb, half), b in bm4.items():
                                wv = WR[:, sl[half], :].rearrange("p (k c) -> p k c", k=8)
                                ins = eng.matmul(banks[b][:, 0:512], XNP[:, k, tb * 128:(tb + 1) * 128], wv[:, k, :],
                                                 start=(k == 0), stop=(k == 7))
                            return ins
                        P.add("pe", v_mm, reads=[f"xnp{k}"] + [f"w{x}" for x in sl], writes=[f"ps{b}" for b in bm4.values()])
                else:
                    for (tb, half), b in bm4.items():
                        def v_mm2(eng, tb=tb, half=half, b=b, sl=sl):
                            wv = WR[:, sl[half], :].rearrange("p (k c) -> p k c", k=8)
                            ins = None
                            for k in range(8):
                                ins = eng.matmul(banks[b][:, 0:512], XNP[:, k, tb * 128:(tb + 1) * 128], wv[:, k, :],
                                                 start=(k == 0), stop=(k == 7))
                            return ins
                        P.add("pe", v_mm2, reads=[f"xnp{k}" for k in range(8)] + [f"w{sl[half]}"], writes=[f"ps{b}"])
                if tbp + 2 >= NTB:
                    release_unit(ubase)
                    release_unit(ubase + 1)
                if tbp == 0:
                    emit_rx()
                    emit_xn()
                    emit_epv()
                v_post2A(tbp, vbank)
                if tbp + 2 >= NTB:
                    emit_epv2()
                    for t2 in range(0, NTB, 2):
                        v_post2B(t2)
            emit_gbA(tl + 1)
            stA = Stat(6)
            for j in range(8):
                jj = j % 4
                bs = []
                bs = [None] * 4
                for si in (0, 2, 3, 1):
                    n = ubase + 2 + (j // 4) * 4 + si
                    s = n % NSLOT
                    b = nb()
                    bs[si] = b

                    def a_mm(eng, s=s, jj=jj, b=b):
                        wv = WR[:, s, :].rearrange("p (k c) -> p k c", k=8)
                        ins = None
                        for k in range(8):
                            ins = eng.matmul(banks[b][:, 0:T], wv[:, k, jj * 128:(jj + 1) * 128], XN[:, k, 0:T],
                                             start=(k == 0), stop=(k == 7))
                        return ins
                    P.add("pe", a_mm, reads=[f"xn{k}" for k in range(8)] + [f"w{s}"], writes=[f"ps{b}"])
                if jj == 3:
                    for si in range(4):
                        release_unit(ubase + 2 + (j // 4) * 4 + si)
                stA.pairs()
                stA.flush(1)
                bh, bb, bc, bz = bs
                fh = nf()
                P.add("act", lambda eng, bh=bh, fh=fh: eng.activation(out=fbuf(fh), in_=banks[bh][:, 0:T], func=AF.Copy),
                      reads=[f"ps{bh}"], writes=[f"f{fh}"])
                fz = nf()
                P.add("act", lambda eng, bz=bz, fz=fz: eng.activation(out=fbuf(fz), in_=banks[bz][:, 0:T], func=AF.Silu),
                      reads=[f"ps{bz}"], writes=[f"f{fz}"])
                fc = nf()
                CH = FT[:, fc, 0:nseg * (L + 2)].rearrange("p (s c) -> p s c", s=nseg)
                if kind == "p":
                    hist = PST[:, l, j, :].unsqueeze(1)
                    hres = f"pst{l}_{j}"
                else:
                    c0 = (l * 8 + j) * 8
                    hist = SST[:, c0:c0 + 8].rearrange("p (s r) -> p s r", s=nseg)
                    hres = "sst"
                P.add("dve", lambda eng, CH=CH, hist=hist: eng.tensor_copy(CH[:, :, 0:2], hist),
                      reads=[hres], writes=[f"f{fc}"])
                P.add("dve", lambda eng, CH=CH, bc=bc, fh=fh: eng.tensor_tensor(
                    out=CH[:, :, 2:2 + L], in0=seg(banks[bc][:, 0:T]), in1=seg(fbuf(fh)), op=ALU.mult),
                    reads=[f"ps{bc}", f"f{fh}"], writes=[f"f{fc}"])
                if kind == "p":
                    P.add("dve", lambda eng, CH=CH, l=l, j=j: eng.tensor_copy(PST[:, l, j, :].unsqueeze(1), CH[:, :, L:L + 2]),
                          reads=[f"f{fc}"], writes=[f"pst{l}_{j}"])
                else:
                    P.add("dve", lambda eng, CH=CH, l=l, j=j: eng.tensor_copy(
                        SNEW[:, l, j, :].rearrange("p (s r) -> p s r", s=nseg), CH[:, :, L:L + 2]),
                        reads=[f"f{fc}"], writes=[f"snew{l}_{j}"])
                fa = nf()
                ACC = seg(fbuf(fa))
                wcol = 32 + (l * 3) * 8 + j
                P.add("dve", lambda eng, CH=CH, ACC=ACC, wcol=wcol: eng.tensor_scalar(
                    out=ACC, in0=CH[:, :, 2:2 + L], scalar1=cc(wcol + 16), scalar2=None, op0=ALU.mult),
                    reads=[f"f{fc}", "cst"], writes=[f"f{fa}"])
                P.add("dve", lambda eng, CH=CH, ACC=ACC, wcol=wcol: eng.scalar_tensor_tensor(
                    out=ACC, in0=CH[:, :, 1:1 + L], scalar=cc(wcol + 8), in1=ACC, op0=ALU.mult, op1=ALU.add),
                    reads=[f"f{fc}", f"f{fa}", "cst"], writes=[f"f{fa}"])
                P.add("dve", lambda eng, CH=CH, ACC=ACC, wcol=wcol: eng.scalar_tensor_tensor(
                    out=ACC, in0=CH[:, :, 0:L], scalar=cc(wcol), in1=ACC, op0=ALU.mult, op1=ALU.add),
                    reads=[f"f{fc}", f"f{fa}", "cst"], writes=[f"f{fa}"])
                P.add("dve", lambda eng, bb=bb, fa=fa: eng.tensor_tensor(out=fbuf(fa), in0=banks[bb][:, 0:T], in1=fbuf(fa), op=ALU.mult),
                      reads=[f"ps{bb}", f"f{fa}"], writes=[f"f{fa}"])
                sqi = nsq()
                P.add("act", lambda eng, fa=fa, sqi=sqi: eng.activation(out=SQ[:, sqi, 0:T], in_=fbuf(fa), func=AF.Square),
                      reads=[f"f{fa}"], writes=[f"sq{sqi}"])
                P.add("dve", lambda eng, fa=fa, fz=fz, j=j, l=l: eng.scalar_tensor_tensor(
                    out=YC[:, j, 0:T], in0=fbuf(fa), scalar=cc(128 + l * 8 + j), in1=fbuf(fz), op0=ALU.mult, op1=ALU.mult),
                    reads=[f"f{fa}", f"f{fz}", "cst"], writes=[f"yc{j}"])
                stA.add(sqi)
            if tidx == 0 and l + 1 < DEPTH:
                sg_prep_pe(l + 1)
                stuff_prep(l + 1)

            stB = Stat(7)
            for g in range(8):
                gg = g % 4
                bu = nb()
                bz = nb()
                bm = nb()
                for si, b in ((0, bu), (1, bz)):
                    n = ubase + 10 + (g // 4) * 2 + si
                    s = n % NSLOT

                    def b_mm(eng, s=s, b=b, gg=gg):
                        wv = WR[:, s, :].rearrange("p (k c) -> p k c", k=8)
                        ins = None
                        for k in range(8):
                            ins = eng.matmul(banks[b][:, 0:T], wv[:, k, gg * 128:(gg + 1) * 128], XN[:, k, 0:T],
                                             start=(k == 0), stop=(k == 7))
                        return ins
                    P.add("pe", b_mm, reads=[f"xn{k}" for k in range(8)] + [f"w{s}"], writes=[f"ps{b}"])
                if gg == 3:
                    for si in range(2):
                        release_unit(ubase + 10 + (g // 4) * 2 + si)

                def m_mm(eng, g=g, bm=bm, l=l):
                    ins = None
                    for tb in range(NTB):
                        ins = eng.matmul(banks[bm][:, tb * 128:(tb + 1) * 128], VN[:, tb, g * 128:(g + 1) * 128],
                                         WMT[:, l, kv, g, :], start=True, stop=True)
                    return ins
                P.add("pe", m_mm, reads=[f"vn{tb}" for tb in range(NTB)] + [f"wmt{l}"], writes=[f"ps{bm}"])
                stA.pairs()
                stA.flush(1 if g == 0 else 0)
                stB.pairs()
                stB.flush(1)
                fz = nf()
                P.add("act", lambda eng, bz=bz, fz=fz: eng.activation(out=fbuf(fz), in_=banks[bz][:, 0:T], func=AF.Silu),
                      reads=[f"ps{bz}"], writes=[f"f{fz}"])
                ft = nf()
                bias = GB[:, 2048 + g * 128:2048 + (g + 1) * 128].unsqueeze(1).broadcast_to([128, NTB, 128])
                if kind == "p":
                    P.add("dve", lambda eng, bm=bm, ft=ft, bias=bias, g=g, l=l: eng.scalar_tensor_tensor(
                        out=fbuf(ft).rearrange("p (a t) -> p a t", a=NTB), in0=banks[bm][:, 0:T].rearrange("p (a t) -> p a t", a=NTB),
                        scalar=cc(200 + l * 8 + g), in1=bias, op0=ALU.mult, op1=ALU.add),
                        reads=[f"ps{bm}", "gbB", "cst"], writes=[f"f{ft}"])
                else:
                    P.add("dve", lambda eng, bm=bm, ft=ft, bias=bias: eng.tensor_tensor(
                        out=fbuf(ft).rearrange("p (a t) -> p a t", a=NTB), in0=banks[bm][:, 0:T].rearrange("p (a t) -> p a t", a=NTB),
                        in1=bias, op=ALU.add), reads=[f"ps{bm}", "gbB"], writes=[f"f{ft}"])
                P.add("dve", lambda eng, bu=bu, ft=ft: eng.tensor_tensor(out=fbuf(ft), in0=banks[bu][:, 0:T], in1=fbuf(ft), op=ALU.mult),
                      reads=[f"ps{bu}", f"f{ft}"], writes=[f"f{ft}"])
                sqi = nsq()
                P.add("act", lambda eng, ft=ft, sqi=sqi: eng.activation(out=SQ[:, sqi, 0:T], in_=fbuf(ft), func=AF.Square),
                      reads=[f"f{ft}"], writes=[f"sq{sqi}"])
                if g == 7:
                    P.add("act", lambda eng: eng.activation(out=JUNK[:, 0:1], in_=ONES[:, 0:1], func=AF.Ln), reads=["ones"])
                P.add("dve", lambda eng, ft=ft, fz=fz, g=g, l=l: eng.scalar_tensor_tensor(
                    out=YC[:, 8 + g, 0:T], in0=fbuf(ft), scalar=cc(160 + l * 8 + g), in1=fbuf(fz), op0=ALU.mult, op1=ALU.mult),
                    reads=[f"f{ft}", f"f{fz}", "cst"], writes=[f"yc{8 + g}"])
                stB.add(sqi)
            assert not stA.raw and not stA.ready and stA.idx == 4
            stB.pairs()
            emit_gbB(tl + 1)
            if l == DEPTH - 1:
                emit_xin_load(tidx + 1)

            stX = Stat(6)
            first = True
            for mp in range(4):
                n = ubase + 14 + mp
                s = n % NSLOT
                ba = [nb(), nb()]
                bbk = [nb(), nb()]
                for part, bks in ((0, ba), (1, bbk)):
                    for mm in range(2):
                        b = bks[mm]

                        def o_mm(eng, s=s, mm=mm, b=b, part=part):
                            wv = WR[:, s, :].rearrange("p (k c) -> p k c", k=16)
                            ins = None
                            for kk in range(8):
                                kc = part * 8 + kk
                                ins = eng.matmul(banks[b][:, 0:T], wv[:, kc, mm * 128:(mm + 1) * 128], YC[:, kc, 0:T],
                                                 start=(kk == 0), stop=(kk == 7))
                            return ins
                        P.add("pe", o_mm, reads=[f"yc{part * 8 + kk}" for kk in range(8)] + [f"w{s}"], writes=[f"ps{b}"])
                    if first and part == 1:
                        stB.flush(0)
                        assert stB.idx == 4 and not stB.raw
                        P.add("act", lambda eng: eng.activation(out=RST[:, 1, 0:T], in_=banks[6][:, 0:T], func=AF.Ln, bias=EPS),
                              reads=["ps6"], writes=["ra"])
                        P.add("act", lambda eng: eng.activation(out=RST[:, 1, 0:T], in_=RST[:, 1, 0:T], func=AF.Exp, scale=-0.5), reads=["ra"], writes=["ra"])
                        P.add("act", lambda eng: eng.activation(out=RST[:, 2, 0:T], in_=banks[7][:, 0:T], func=AF.Ln, bias=EPS),
                              reads=["ps7"], writes=["rb"])
                        P.add("act", lambda eng: eng.activation(out=RST[:, 2, 0:T], in_=RST[:, 2, 0:T], func=AF.Exp, scale=-0.5), reads=["rb"], writes=["rb"])
                        first = False
                release_unit(n)
                f1s = [nf(), nf()]
                f2s = [nf(), nf()]
                for mm in range(2):
                    P.add("dve", lambda eng, b=ba[mm], f1=f1s[mm]: eng.tensor_tensor(out=fbuf(f1), in0=banks[b][:, 0:T], in1=RST[:, 1, 0:T], op=ALU.mult),
                          reads=[f"ps{ba[mm]}", "ra"], writes=[f"f{f1s[mm]}"])
                for mm in range(2):
                    P.add("dve", lambda eng, b=bbk[mm], f2=f2s[mm]: eng.tensor_tensor(out=fbuf(f2), in0=banks[b][:, 0:T], in1=RST[:, 2, 0:T], op=ALU.mult),
                          reads=[f"ps{bbk[mm]}", "rb"], writes=[f"f{f2s[mm]}"])
                for mm in range(2):
                    m = mp * 2 + mm
                    f1 = f1s[mm]
                    f2 = f2s[mm]
                    ae = "dve" if mp == 3 else pe_
                    P.add(ae, lambda eng, f1=f1, f2=f2: eng.tensor_tensor(out=fbuf(f1), in0=fbuf(f1), in1=fbuf(f2), op=ALU.add),
                          reads=[f"f{f1}", f"f{f2}"], writes=[f"f{f1}"])
                    P.add(ae, lambda eng, f1=f1, m=m: eng.tensor_tensor(out=X[:, m, 0:T], in0=X[:, m, 0:T], in1=fbuf(f1), op=ALU.add),
                          reads=[f"f{f1}", f"x{m}"], writes=[f"x{m}"])
                    sqi = nsq()
                    P.add("act", lambda eng, m=m, sqi=sqi: eng.activation(out=SQ[:, sqi, 0:T], in_=X[:, m, 0:T], func=AF.Square),
                          reads=[f"x{m}"], writes=[f"sq{sqi}"])
                    if l + 1 < DEPTH:
                        P.add("act", lambda eng, m=m, l=l: eng.activation(out=XNP[:, m, 0:T], in_=X[:, m, 0:T], func=AF.Identity,
                                                                         scale=cc((l + 1) * 8 + m)),
                              reads=[f"x{m}", "cst"], writes=[f"xnp{m}"])
                    stX.add(sqi)
                stX.flush(0)
                stX.pairs(hold=2)
            if l == DEPTH - 1:
                stX.drain()
            else:
                carry[0] = stX

        P.add("act", lambda eng: eng.activation(out=RST[:, 0, 0:T], in_=banks[6][:, 0:T], func=AF.Ln, bias=EPS),
              reads=["ps6"], writes=["rx"])
        P.add("act", lambda eng: eng.activation(out=RST[:, 0, 0:T], in_=RST[:, 0, 0:T], func=AF.Exp, scale=-0.5), reads=["rx"], writes=["rx"])
        fo = []
        for k in range(8):
            f = nf()
            fo.append(f)
            P.add("dve", lambda eng, k=k, f=f: eng.scalar_tensor_tensor(
                out=FT[:, f, 0:T], in0=X[:, k, 0:T], scalar=cc(192 + k), in1=RST[:, 0, 0:T], op0=ALU.mult, op1=ALU.mult),
                reads=[f"x{k}", "rx", "cst"], writes=[f"f{f}"])
        if tidx + 1 < len(tiles):
            if tiles[tidx + 1][0] == "s":
                sst_prep()
            prologue(tidx + 1)
        for tb in range(NTB):
            for half in range(2):
                b = nb()

                def tr_out(eng, tb=tb, half=half, b=b):
                    ins = None
                    for kk in range(4):
                        k = half * 4 + kk
                        ins = eng.transpose(banks[b][:, kk * 128:(kk + 1) * 128], FT[:, fo[k], tb * 128:(tb + 1) * 128], IDENT)
                    return ins
                P.add("pe", tr_out, reads=[f"f{fo[half * 4 + kk]}" for kk in range(4)] + ["ident"], writes=[f"ps{b}"])
                P.add("act", lambda eng, tb=tb, half=half, b=b: eng.activation(
                    out=XOUT[:, tb, half * 512:(half + 1) * 512], in_=banks[b][:, 0:512], func=AF.Copy),
                    reads=[f"ps{b}"], writes=XOUT_RES[tb])
        ores = []
        for tb in range(NTB):
            ores += XOUT_RES[tb]
        if kind == "p":
            dst = yp[ti * 512:(ti + 1) * 512, :].rearrange("(tb p) d -> p tb d", p=128)
        else:
            dst = ys.rearrange("(tb p) d -> p tb d", p=128)
        P.add("sp", lambda eng: [eng.dma_start(out=dst, in_=XOUT[:, 0:NTB, :])], reads=ores, dma="xoutst")

    def emit_state_out(kind):
        nrow = 2 if kind == "p" else 8
        for l in range(DEPTH):
            for hh in range(2):
                b = nb()

                def st_tr(eng, l=l, b=b, hh=hh):
                    ins = None
                    for jj in range(4):
                        j = hh * 4 + jj
                        src = PST[:, l, j, :] if kind == "p" else SNEW[:, l, j, :]
                        ins = eng.transpose(banks[b][0:nrow, jj * 128:(jj + 1) * 128], src, IDENT)
                    return ins
                rr = ([f"pst{l}_{hh * 4 + jj}" for jj in range(4)] if kind == "p"
                      else [f"snew{l}_{hh * 4 + jj}" for jj in range(4)])
                P.add("pe", st_tr, reads=rr + ["ident"], writes=[f"ps{b}"])
                P.add("act", lambda eng, b=b, hh=hh: eng.activation(
                    out=OSTG[0:nrow, hh * 512:(hh + 1) * 512], in_=banks[b][0:nrow, 0:512], func=AF.Copy),
                    reads=[f"ps{b}"], writes=["ostg"])
            dst = ncp[l] if kind == "p" else ncs[l]
            P.add("sp", lambda eng, dst=dst: [eng.dma_start(out=dst, in_=OSTG[0:nrow, :])], reads=["ostg"], dma="ostst")

    prologue(0)
    for tidx in range(len(tiles)):
        do_tile(tidx)
        kind, ti = tiles[tidx]
        if (kind == "p" and ti == NT - 1) or kind == "s":
            emit_state_out(kind)

    P.finalize()
    with ExitStack() as es:
        sems = {}
        for i, s in enumerate(P.sem_names()):
            sems[s] = es.enter_context(nc.semaphore(f"s{i}"))
        P.emit(nc, sems)
    return nc


_NC_CACHE = {}


def _get_nc(NT=16):
    if NT not in _NC_CACHE:
        _NC_CACHE[NT] = build_nc(NT)
    return _NC_CACHE[NT]


def make_in_maps(inputs, NT=16, n_cores=N_CORES):
    f = lambda a: np.ascontiguousarray(np.asarray(a, dtype=np.float32))
    shared = {k: f(inputs[k]) for k in ("norm_g", "w_in", "conv_w", "sg_ln_g", "sg_ln_b", "sg_w", "sg_b",
                                        "out_g_conv", "out_g_sgu", "w_out", "final_g")}
    xpr = np.asarray(inputs["x_prompt"], dtype=np.float32)
    xsa = np.asarray(inputs["x_sample"], dtype=np.float32)
    stc = np.asarray(inputs["state_conv"], dtype=np.float32)
    maps = []
    for c in range(n_cores):
        m = dict(shared)
        m["xp"] = np.ascontiguousarray(xpr[c, :NT * 512])
        m["xs"] = np.ascontiguousarray(xsa[NSEQ_S * c:NSEQ_S * (c + 1)].reshape(NSEQ_S * LS, D))
        m["sc"] = np.ascontiguousarray(stc[:, NSEQ_S * c:NSEQ_S * (c + 1)])
        maps.append(m)
    return maps


def kernel(x_prompt, x_sample, state_conv, norm_g, w_in, conv_w, sg_ln_g, sg_ln_b,
           sg_w, sg_b, out_g_conv, out_g_sgu, w_out, final_g):
    inputs = dict(x_prompt=x_prompt, x_sample=x_sample, state_conv=state_conv, norm_g=norm_g, w_in=w_in,
                  conv_w=conv_w, sg_ln_g=sg_ln_g, sg_ln_b=sg_ln_b, sg_w=sg_w, sg_b=sg_b,
                  out_g_conv=out_g_conv, out_g_sgu=out_g_sgu, w_out=w_out, final_g=final_g)
    nc = _get_nc(16)
    in_maps = make_in_maps(inputs, 16)
    res = run_bass_kernel_spmd(nc, in_maps, core_ids=list(range(N_CORES)))
    B = N_CORES
    y_prompt = np.empty((B, SEQ, D), np.float32)
    y_sample = np.empty((B * NSEQ_S, LS, D), np.float32)
    ncp_o = np.empty((DEPTH, B, 2, D), np.float32)
    ncs_o = np.empty((DEPTH, B * NSEQ_S, 2, D), np.float32)
    nv_o = np.empty((DEPTH, B * NSEQ_S, LS, D), np.float32)
    for c, r in enumerate(res.results):
        y_prompt[c] = r["yp"]
        y_sample[NSEQ_S * c:NSEQ_S * (c + 1)] = r["ys"].reshape(NSEQ_S, LS, D)
        ncp_o[:, c] = r["ncp"]
        ncs_o[:, NSEQ_S * c:NSEQ_S * (c + 1)] = r["ncs"].reshape(DEPTH, NSEQ_S, 2, D)
        nv_o[:, NSEQ_S * c:NSEQ_S * (c + 1)] = r["nv"].reshape(DEPTH, NSEQ_S, LS, D)
    return (y_prompt, y_sample, ncp_o, ncs_o, nv_o)
```
